# Optimizing a Trainium2 kernel written in Bass

```python
import jax, jax.numpy as jnp
from jax import lax
import numpy as np

D_MODEL = 1024
BATCH = 16
SEQ = 2048
DEPTH = 1

D_FF = 2816
N_HEADS = 8
QK_NOPE_DIM = 64
QK_ROPE_DIM = 32
QK_HEAD_DIM = QK_NOPE_DIM + QK_ROPE_DIM
V_HEAD_DIM = 64
Q_LORA_RANK = 384
KV_LORA_RANK = 256
CONV_DIM = 1024
CONV_WIDTH = 3
ROPE_THETA = 10000.0
Q_BLOCK = 128
NORM_EPS = 1e-6
ATTN_OUT_DIM = N_HEADS * V_HEAD_DIM
N_BRANCHES = 2
IN_SIZES = (Q_LORA_RANK, KV_LORA_RANK, QK_ROPE_DIM, CONV_DIM, CONV_DIM, CONV_DIM, D_MODEL, D_MODEL)
IN_DIM = sum(IN_SIZES)
IN_SPLITS = tuple(int(s) for s in np.cumsum(IN_SIZES)[:-1])

kernel_name = "hybrid_mla_shortconv_macaron_block"


def rmsnorm(x, gain):
    x32 = x.astype(jnp.float32)
    y = x32 * lax.rsqrt(jnp.mean(x32 * x32, axis=-1, keepdims=True) + NORM_EPS)
    return (y * gain.astype(jnp.float32)).astype(x.dtype)


def swiglu(h, w_gate, w_up, w_down):
    return (jax.nn.silu(h @ w_gate) * (h @ w_up)) @ w_down


def rope(t, positions):
    half = QK_ROPE_DIM // 2
    inv_freq = 1.0 / (ROPE_THETA ** (jnp.arange(half, dtype=jnp.float32) / half))
    ang = positions.astype(jnp.float32)[..., None] * inv_freq
    ang = ang.reshape(ang.shape[:2] + (1,) * (t.ndim - 3) + (half,))
    cos, sin = jnp.cos(ang).astype(t.dtype), jnp.sin(ang).astype(t.dtype)
    t1, t2 = t[..., :half], t[..., half:]
    return jnp.concatenate([t1 * cos - t2 * sin, t1 * sin + t2 * cos], axis=-1)


def causal_block_attention(q, k, v):
    b, s, h, dq = q.shape
    nb = s // Q_BLOCK
    scale = QK_HEAD_DIM ** -0.5
    q_blocks = q.reshape(b, nb, Q_BLOCK, h, dq).transpose(1, 0, 2, 3, 4)
    key_pos = jnp.arange(s)

    def one_block(args):
        qb, blk = args
        scores = jnp.einsum('bqhd,bkhd->bhqk', qb, k).astype(jnp.float32) * scale
        q_pos = blk * Q_BLOCK + jnp.arange(Q_BLOCK)
        mask = key_pos[None, :] <= q_pos[:, None]
        scores = jnp.where(mask[None, None], scores, -1e30)
        p = jax.nn.softmax(scores, axis=-1).astype(v.dtype)
        return jnp.einsum('bhqk,bkhd->bqhd', p, v)

    out = lax.map(one_block, (q_blocks, jnp.arange(nb)))
    return out.transpose(1, 0, 2, 3, 4).reshape(b, s, h * V_HEAD_DIM)


def setup_inputs(seed: int = 0) -> dict:
    key = jax.random.key(seed)
    ks = jax.random.split(key, 32)

    def w(k, shape, fan_in):
        return jax.random.normal(k, shape, jnp.float32) * fan_in ** -0.5

    def gain(k, n):
        return 1.0 + 0.02 * jax.random.normal(k, (n,), jnp.float32)

    positions = jnp.broadcast_to(jnp.arange(SEQ, dtype=jnp.int32)[None, :], (BATCH, SEQ))
    return {
        "x": jax.random.normal(ks[0], (BATCH, SEQ, D_MODEL), jnp.float32),
        "positions": positions,
        "ffn1_norm": gain(ks[1], D_MODEL),
        "ffn1_w_gate": w(ks[2], (D_MODEL, D_FF), D_MODEL),
        "ffn1_w_up": w(ks[3], (D_MODEL, D_FF), D_MODEL),
        "ffn1_w_down": w(ks[4], (D_FF, D_MODEL), D_FF),
        "mix_norm": gain(ks[5], D_MODEL),
        "w_in": w(ks[6], (D_MODEL, IN_DIM), D_MODEL),
        "gate_bias": 0.01 * jax.random.normal(ks[7], (N_BRANCHES * D_MODEL,), jnp.float32),
        "q_a_norm": gain(ks[8], Q_LORA_RANK),
        "w_uq": w(ks[9], (Q_LORA_RANK, N_HEADS * QK_HEAD_DIM), Q_LORA_RANK),
        "kv_a_norm": gain(ks[10], KV_LORA_RANK),
        "w_uk": w(ks[11], (KV_LORA_RANK, N_HEADS * QK_NOPE_DIM), KV_LORA_RANK),
        "w_uv": w(ks[12], (KV_LORA_RANK, N_HEADS * V_HEAD_DIM), KV_LORA_RANK),
        "q_head_norm": gain(ks[13], QK_HEAD_DIM),
        "k_head_norm": gain(ks[14], QK_HEAD_DIM),
        "w_proj_attn": w(ks[15], (ATTN_OUT_DIM, D_MODEL), ATTN_OUT_DIM),
        "conv_w": w(ks[16], (CONV_WIDTH, CONV_DIM), CONV_WIDTH),
        "w_proj_conv": w(ks[17], (CONV_DIM, D_MODEL), CONV_DIM),
        "w_out": w(ks[18], (D_MODEL, D_MODEL), D_MODEL),
        "ffn2_norm": gain(ks[19], D_MODEL),
        "ffn2_w_gate": w(ks[20], (D_MODEL, D_FF), D_MODEL),
        "ffn2_w_up": w(ks[21], (D_MODEL, D_FF), D_MODEL),
        "ffn2_w_down": w(ks[22], (D_FF, D_MODEL), D_FF),
    }


def reference(x, positions, ffn1_norm, ffn1_w_gate, ffn1_w_up, ffn1_w_down,
              mix_norm, w_in, gate_bias, q_a_norm, w_uq, kv_a_norm, w_uk, w_uv,
              q_head_norm, k_head_norm, w_proj_attn, conv_w, w_proj_conv, w_out,
              ffn2_norm, ffn2_w_gate, ffn2_w_up, ffn2_w_down):
    b, s, _ = x.shape
    for _layer in range(DEPTH):
        x = x + 0.5 * swiglu(rmsnorm(x, ffn1_norm), ffn1_w_gate, ffn1_w_up, ffn1_w_down)

        h = rmsnorm(x, mix_norm)
        proj = h @ w_in
        q_lat, kv_lat, k_rope_raw, xc, gB, gC, gate_logits = jnp.split(proj, IN_SPLITS, axis=-1)[:7] + []  if False else jnp.split(proj, IN_SPLITS, axis=-1)[:7]
        gate_logits = proj[..., IN_SPLITS[-1]:] if False else jnp.concatenate([gate_logits, proj[..., IN_SPLITS[-1]:]], axis=-1)

        q = (rmsnorm(q_lat, q_a_norm) @ w_uq).reshape(b, s, N_HEADS, QK_HEAD_DIM)
        c_kv = rmsnorm(kv_lat, kv_a_norm)
        k_nope = (c_kv @ w_uk).reshape(b, s, N_HEADS, QK_NOPE_DIM)
        v = (c_kv @ w_uv).reshape(b, s, N_HEADS, V_HEAD_DIM)
        k_rope = jnp.broadcast_to(k_rope_raw[:, :, None, :], (b, s, N_HEADS, QK_ROPE_DIM))
        k = jnp.concatenate([k_nope, k_rope], axis=-1)
        q = rmsnorm(q, q_head_norm)
        k = rmsnorm(k, k_head_norm)
        q = jnp.concatenate([q[..., :QK_NOPE_DIM], rope(q[..., QK_NOPE_DIM:], positions)], axis=-1)
        k = jnp.concatenate([k[..., :QK_NOPE_DIM], rope(k[..., QK_NOPE_DIM:], positions)], axis=-1)
        y_a = causal_block_attention(q, k, v) @ w_proj_attn

        u = gC * xc
        up = jnp.pad(u, ((0, 0), (CONV_WIDTH - 1, 0), (0, 0)))
        z = conv_w[0] * up[:, :s] + conv_w[1] * up[:, 1:s + 1] + conv_w[2] * up[:, 2:s + 2]
        y_b = (gB * z) @ w_proj_conv

        gates = jax.nn.sigmoid(gate_logits + gate_bias)
        merged = gates[..., :D_MODEL] * y_a + gates[..., D_MODEL:] * y_b
        x = x + merged @ w_out

        x = x + 0.5 * swiglu(rmsnorm(x, ffn2_norm), ffn2_w_gate, ffn2_w_up, ffn2_w_down)
    return x
```

```python
import math
import numpy as np
import concourse.bass as bass
import concourse.mybir as mybir
from concourse.bass_utils import run_bass_kernel_spmd

F32 = mybir.dt.float32
BF16 = mybir.dt.bfloat16
I32 = mybir.dt.int32
AF = mybir.ActivationFunctionType
ALU = mybir.AluOpType
AX = mybir.AxisListType

NCORES = 8
S = 2048
D = 1024
DFF = 2816
NH = 8
DQK = 96
QL = 384
KVL = 256
TOK = 4096
G = 512
TPG = G // 128
NG = TOK // G
GPS = S // G
NTILE = TOK // 128
NFC = DFF // 128
EPS = 1e-6
NSLOT = 4
SLOTF = 5632
SM_SHIFT = math.sqrt(96.0)
KR = [(0, 4), (4, 11), (11, 22)]
NDELAY = 2

P_G1, P_GM, P_G2 = 0, 8, 16
P_GQ, P_GKV = 24, 27
P_GQH, P_GKH = 29, 125
P_GB = 221
P_CW = 237
P_INVF = 261
P_ID = 277
P_TRI = 405
P_GQC = 533
P_GKC = 534
NPAR = 535


class Dep:
    __slots__ = ("name", "w", "r", "excl")

    def __init__(self, name="", excl=False):
        self.name = name
        self.w = {}
        self.r = {}
        self.excl = excl


class Eng:
    def __init__(self, name):
        self.name = name
        self.sem = None
        self.count = 0
        self.ops = []
        self.known = {}


class FW:
    def __init__(self, nc):
        self.nc = nc
        self.sems = {}
        self.E = {k: Eng(k) for k in ("pe", "act", "dve", "pool", "sp")}
        for k, e in self.E.items():
            if k != "sp":
                e.sem = self.new_sem("e_" + k)
        self.dma_vals = {}
        self.nwaits = 0
        self.nops = 0

    def new_sem(self, name):
        self.sems[name] = self.nc.alloc_semaphore(name)
        return name

    def new_dma_sem(self, name):
        k = self.new_sem("d_" + name)
        self.dma_vals[k] = 0
        return k

    def _collect(self, eng, reads, writes):
        deps = {}

        def add(semk, val):
            if deps.get(semk, 0) < val:
                deps[semk] = val

        for d in reads:
            for semk, (val, en) in d.w.items():
                if en == eng.name and eng.name == "pe":
                    continue
                add(semk, val)
            if d.excl:
                for semk, (val, en) in d.r.items():
                    if en != eng.name:
                        add(semk, val)
        for d in writes:
            for semk, (val, en) in d.w.items():
                if en == eng.name and eng.name == "pe":
                    continue
                add(semk, val)
            for semk, (val, en) in d.r.items():
                if en == eng.name and eng.name == "pe":
                    continue
                add(semk, val)
        return deps

    def _emit_waits(self, eng, deps):
        for semk, val in deps.items():
            if eng.known.get(semk, 0) >= val:
                continue
            for e2 in self.E.values():
                if e2.sem == semk:
                    assert val <= e2.count, f"wait on unsignaled event {semk} {val}>{e2.count}"
            eng.known[semk] = val
            h = self.sems[semk]
            eng.ops.append(lambda e, h=h, val=val: e.wait_ge(h, val))
            self.nwaits += 1

    @staticmethod
    def _flat(lst):
        out = []
        for d in lst:
            if isinstance(d, (list, tuple)):
                out.extend(FW._flat(d))
            else:
                out.append(d)
        return out

    def op(self, engname, fn, reads=(), writes=(), signal=True):
        eng = self.E[engname]
        reads = self._flat(reads)
        writes = self._flat(writes)
        self._emit_waits(eng, self._collect(eng, reads, writes))
        self.nops += 1
        if signal:
            eng.count += 1
            val = eng.count
            h = self.sems[eng.sem]
            eng.ops.append(lambda e, fn=fn, h=h: fn(e).then_inc(h, 1))
        else:
            val = eng.count + 1
            eng.ops.append(lambda e, fn=fn: fn(e))
        ev = (val, eng.name)
        for d in writes:
            d.w = {eng.sem: ev}
            d.r = {}
        for d in reads:
            if any(d is w_ for w_ in writes):
                continue
            d.r[eng.sem] = ev
        return ev

    def dma(self, issuer, semk, fns, reads=(), writes=()):
        eng = self.E[issuer]
        reads = self._flat(reads)
        writes = self._flat(writes)
        self._emit_waits(eng, self._collect(eng, reads, writes))
        h = self.sems[semk]
        for fn in fns:
            self.dma_vals[semk] += 16
            eng.ops.append(lambda e, fn=fn, h=h: fn(e).then_inc(h, 16))
        ev = (self.dma_vals[semk], "dma:" + semk)
        for d in writes:
            d.w = {semk: ev}
            d.r = {}
        for d in reads:
            if any(d is w_ for w_ in writes):
                continue
            d.r[semk] = ev
        return ev

    def final_wait(self, engname, semks):
        eng = self.E[engname]
        for semk in semks:
            h = self.sems[semk]
            v = self.dma_vals[semk]
            if v > 0:
                eng.ops.append(lambda e, h=h, v=v: e.wait_ge(h, v))

    def replay(self):
        with self.nc.Block() as block:
            @block.tensor
            def _(e):
                for f in self.E["pe"].ops:
                    f(e)

            @block.scalar
            def _(e):
                for f in self.E["act"].ops:
                    f(e)

            @block.vector
            def _(e):
                for f in self.E["dve"].ops:
                    f(e)

            @block.gpsimd
            def _(e):
                for f in self.E["pool"].ops:
                    f(e)

            @block.sync
            def _(e):
                for f in self.E["sp"].ops:
                    f(e)


def weight_plan():
    pl = []
    for f in ("f1", "f2"):
        sub = []
        for c in range(NFC):
            sub.append((f"{f}gu{c}", 2 * 8 * 128))
        for hf in range(2):
            for kh, (k0, k1) in enumerate(KR):
                sub.append((f"{f}d{hf}{kh}", (k1 - k0) * 512))
        if f == "f1":
            pl += sub
            pl.append(("lat", 8 * 672))
            pl.append(("uqkv", 3 * 768 + 2 * 1024))
            for c in range(8):
                pl.append((f"conv{c}", 8 * 3 * 128))
            for c in range(8):
                pl.append((f"m4{c}", 28 * 128))
            for hf in range(2):
                pl.append((f"wo{hf}", 8 * 512))
        else:
            pl += sub
    return pl


def pack_weights(inp):
    f32 = np.float32

    def kt(w):
        K, N = w.shape
        return w.reshape(K // 128, 128, N).transpose(1, 0, 2)

    pieces = {}
    for f, pre in (("f1", "ffn1"), ("f2", "ffn2")):
        wg = kt(np.asarray(inp[pre + "_w_gate"], f32))
        wu = kt(np.asarray(inp[pre + "_w_up"], f32))
        wd = kt(np.asarray(inp[pre + "_w_down"], f32))
        for c in range(NFC):
            a = np.stack([wg[:, :, c * 128:(c + 1) * 128], wu[:, :, c * 128:(c + 1) * 128]], axis=1)
            pieces[f"{f}gu{c}"] = a.reshape(128, -1)
        for hf in range(2):
            for kh, (k0, k1) in enumerate(KR):
                pieces[f"{f}d{hf}{kh}"] = wd[:, k0:k1, hf * 512:(hf + 1) * 512].reshape(128, -1)
    w_in = kt(np.asarray(inp["w_in"], f32))
    pieces["lat"] = w_in[:, :, 0:672].reshape(128, -1)
    pieces["uq"] = kt(np.asarray(inp["w_uq"], f32)).reshape(128, -1)
    ukv = np.concatenate([np.asarray(inp["w_uk"], f32), np.asarray(inp["w_uv"], f32)], axis=1)
    pieces["ukv"] = kt(ukv).reshape(128, -1)
    pieces["uqkv"] = np.concatenate([pieces["uq"], pieces["ukv"]], axis=1)
    o_xc, o_gb, o_gc, o_gl = 672, 672 + 1024, 672 + 2048, 672 + 3072
    for c in range(8):
        a = np.stack([w_in[:, :, o + c * 128:o + (c + 1) * 128] for o in (o_xc, o_gb, o_gc)], axis=2)
        pieces[f"conv{c}"] = a.reshape(128, -1)
    wpc = kt(np.asarray(inp["w_proj_conv"], f32))
    wpa = kt(np.asarray(inp["w_proj_attn"], f32))
    for c in range(8):
        sl = slice(c * 128, (c + 1) * 128)
        a = np.concatenate([wpc[:, :, sl], wpa[:, :, sl],
                            w_in[:, :, o_gl + c * 128:o_gl + (c + 1) * 128],
                            w_in[:, :, o_gl + 1024 + c * 128:o_gl + 1024 + (c + 1) * 128]], axis=1)
        pieces[f"m4{c}"] = a.reshape(128, -1)
    wo = kt(np.asarray(inp["w_out"], f32))
    for hf in range(2):
        pieces[f"wo{hf}"] = wo[:, :, hf * 512:(hf + 1) * 512].reshape(128, -1)
    plan = weight_plan()
    arr = np.concatenate([pieces[n] for n, _ in plan], axis=1)
    for (n, F) in plan:
        assert pieces[n].shape[1] == F, (n, pieces[n].shape, F)
    return np.ascontiguousarray(arr, dtype=f32)


def pack_params(inp):
    f32 = np.float32
    par = np.zeros((128, NPAR), f32)

    def col(v, n):
        return np.asarray(v, f32).reshape(n, 128).T

    par[:, P_G1:P_G1 + 8] = col(inp["ffn1_norm"], 8)
    par[:, P_GM:P_GM + 8] = col(inp["mix_norm"], 8)
    par[:, P_G2:P_G2 + 8] = col(inp["ffn2_norm"], 8)
    par[:, P_GQ:P_GQ + 3] = col(inp["q_a_norm"], 3)
    par[:, P_GKV:P_GKV + 2] = col(inp["kv_a_norm"], 2)
    par[:, P_GQH:P_GQH + 96] = np.asarray(inp["q_head_norm"], f32)[None, :]
    par[:, P_GKH:P_GKH + 96] = np.asarray(inp["k_head_norm"], f32)[None, :]
    par[:, P_GB:P_GB + 16] = col(inp["gate_bias"], 16)
    cw = np.asarray(inp["conv_w"], f32)
    par[:, P_CW:P_CW + 24] = cw.reshape(3, 8, 128).transpose(2, 1, 0).reshape(128, 24)
    half = 16
    invf = (1.0 / (10000.0 ** (np.arange(half, dtype=np.float32) / np.float32(half)))).astype(f32)
    par[:, P_INVF:P_INVF + 16] = invf[None, :]
    par[:, P_ID:P_ID + 128] = np.eye(128, dtype=f32)
    par[:, P_TRI:P_TRI + 128] = np.triu(np.ones((128, 128), f32))
    par[:, P_GQC] = 1.0
    par[:, P_GKC] = 1.0
    par[0:64, P_GQC] = np.asarray(inp["q_head_norm"], f32)[0:64]
    par[0:64, P_GKC] = np.asarray(inp["k_head_norm"], f32)[0:64]
    return par


class StopBuild(Exception):
    pass


def build_program(ng=NG, dbg=False, stop=None):
    stage = {"n": 0}

    def checkpoint(name):
        if stop is not None and name == stop:
            raise StopBuild()

    nc = bass.Bass("TRN2", target_bir_lowering=False)
    plan = weight_plan()
    offs = []
    o = 0
    for n, F in plan:
        offs.append(o)
        o += F
    WTOT = o

    x_d = nc.dram_tensor("x", [TOK, D], F32, kind="ExternalInput").ap()
    pos_d = nc.dram_tensor("pos", [128, NTILE], I32, kind="ExternalInput").ap()
    par_d = nc.dram_tensor("par", [128, NPAR], F32, kind="ExternalInput").ap()
    ws_d = nc.dram_tensor("ws", [128, WTOT], F32, kind="ExternalInput").ap()
    out_d = nc.dram_tensor("out", [TOK, D], F32, kind="ExternalOutput").ap()
    if dbg:
        dbg_d = nc.dram_tensor("dbg", [4, G, D], F32, kind="ExternalOutput").ap()
        dbg_attn = nc.dram_tensor("dbg_attn", [128, TPG, 512], BF16, kind="ExternalOutput").ap()
        dbg_q = nc.dram_tensor("dbg_q", [96, NH, G], BF16, kind="ExternalOutput").ap()
        dbg_k = nc.dram_tensor("dbg_k", [96, NH, G], BF16, kind="ExternalOutput").ap()
        dbg_v = nc.dram_tensor("dbg_v", [128, TPG, NH, 65], BF16, kind="ExternalOutput").ap()

    fw = FW(nc)
    sb = nc.alloc_sbuf_tensor
    op = fw.op

    par = sb("par_sb", [128, NPAR], F32)
    d_par = Dep("par")
    pos_t = sb("pos_sb", [128, NTILE], I32)
    d_pos = Dep("pos")
    identb = sb("identb", [128, 128], BF16)
    d_id = Dep("id")
    trib = sb("trib", [128, 128], BF16)
    d_tri = Dep("tri")
    sinT = sb("sinT", [128, NTILE, 16], F32)
    cosT = sb("cosT", [128, NTILE, 16], F32)
    d_sin, d_cos = Dep("sin"), Dep("cos")
    negM = sb("negM", [128, 1], F32)
    d_negM = Dep("negM")
    epsT = sb("epsT", [128, 1], F32)
    d_epsT = Dep("epsT")

    xres = [sb(f"xres{i}", [128, TPG, D], F32) for i in range(2)]
    d_x = [[Dep(f"x{i}_{t}") for t in range(TPG)] for i in range(2)]
    hT = sb("hT", [128, 8, G], BF16)
    d_hT = [Dep(f"hT{t}") for t in range(TPG)]
    actT = sb("actT", [128, NFC, G], BF16)
    d_act = [Dep(f"act{c}") for c in range(NFC)]
    KT = sb("KT", [128, NH, S], BF16)
    d_KT = [Dep(f"KT{t}") for t in range(S // 128)]
    Vc = sb("Vc", [128, S // 128, NH, 65], BF16)
    d_V = [Dep(f"V{t}") for t in range(S // 128)]
    QT = sb("QT", [128, NH, G], BF16)
    d_QT = [Dep(f"QT{t}") for t in range(TPG)]
    attn_tok = sb("attn_tok", [128, TPG, 512], BF16)
    d_atok = [Dep(f"atok{t}") for t in range(TPG)]
    attnT = sb("attnT", [128, 4, G], BF16)
    d_attnT = [Dep(f"attnT{t}") for t in range(TPG)]
    convT = sb("convT", [128, 8, G], BF16)
    d_conv = [Dep(f"conv{c}") for c in range(8)]
    mrgT = QT
    d_mrg = [[Dep(f"mrg{c}")] + d_QT for c in range(8)]
    carry = sb("carry", [128, 8, 2], F32)
    d_carry = [Dep(f"carry{c}") for c in range(8)]

    wslot = [sb(f"wslot{i}", [128, SLOTF], BF16) for i in range(NSLOT)]
    d_w = [Dep(f"w{i}") for i in range(NSLOT)]
    s_w = [fw.new_dma_sem(f"w{i}") for i in range(NSLOT)]

    junk = sb("junk", [128, D], BF16)
    d_junk = Dep("junk")
    ss = sb("ss", [128, 16], F32)
    d_ss = Dep("ss")
    rr = sb("rr", [128, 16], F32)
    d_rr = Dep("rr")
    n_ra = sb("n_ra", [128, 16], F32)
    n_rb = sb("n_rb", [128, 16], F32)
    n_ri = sb("n_ri", [128, 16], I32)
    d_nra, d_nrb, d_nri = Dep("nra"), Dep("nrb"), Dep("nri")
    xn = [sb(f"xn{i}", [128, D], BF16) for i in range(2)]
    d_xn = [Dep(f"xn{i}") for i in range(2)]
    sg = [sb(f"sg{i}", [128, 512], F32) for i in range(2)]
    d_sg = [Dep(f"sg{i}") for i in range(2)]
    ssl2 = [sb(f"ssl{i}", [128, 4], F32) for i in range(2)]
    d_ssl2 = [Dep(f"ssl{i}") for i in range(2)]
    rl2 = [sb(f"rl{i}", [128, 4], F32) for i in range(2)]
    d_rl2 = [Dep(f"rl{i}") for i in range(2)]
    krf2 = [sb(f"krf{i}", [128, 32], F32) for i in range(2)]
    d_krf2 = [Dep(f"krf{i}") for i in range(2)]
    qlb = sb("qlb", [128, QL], BF16)
    kvb = sb("kvb", [128, KVL], BF16)
    krf = sb("krf", [128, 32], F32)
    d_qlb, d_kvb, d_krf = Dep("qlb"), Dep("kvb"), Dep("krf")
    lt5 = sb("lt5", [128, 5, 128], BF16)
    d_lt5 = Dep("lt5")
    def act_view(c0, nch, dtype, n):
        ap = actT[:, c0:c0 + nch, :].rearrange("p a b -> p (a b)")
        if dtype == F32:
            ap = ap.bitcast(F32)
        return ap[:, 0:n], [d_act[c] for c in range(c0, c0 + nch)]

    qs, d_qs = act_view(0, 3, F32, 768)
    ks, d_ks = act_view(3, 2, F32, 512)
    SQ_f, d_SQ = act_view(5, 6, F32, 16 * 96)
    qr_f, d_qr = act_view(11, 1, F32, 256)
    qf_f, d_qf = act_view(12, 2, BF16, 768)
    kf_f, d_kf = act_view(14, 2, BF16, 768)
    ssh = sb("ssh", [128, 16], F32)
    rh = sb("rh", [128, 16], F32)
    d_ssh, d_rh = Dep("ssh"), Dep("rh")
    d_sshq, d_sshk = Dep("sshq"), Dep("sshk")
    SQ = SQ_f.rearrange("p (h d) -> p h d", h=16)
    qr = qr_f.rearrange("p (h d) -> p h d", h=NH)
    qf = qf_f.rearrange("p (h d) -> p h d", h=NH)
    kf = kf_f.rearrange("p (h d) -> p h d", h=NH)
    rt = [sb(f"rt{i}", [128, NH, 16], F32) for i in range(4)]
    d_rt = [Dep(f"rt{i}") for i in range(4)]
    kb = sb("kb", [128, 32], F32)
    kbr = sb("kbr", [128, 32], F32)
    d_kb, d_kbr = Dep("kb"), Dep("kbr")
    kt4 = [sb(f"kt4{i}", [128, 16], F32) for i in range(4)]
    d_kt4 = [Dep(f"kt4{i}") for i in range(4)]
    pt, d_pt = [], []
    for i in range(4):
        a_, d_ = act_view(18 + i, 1, BF16, 512)
        pt.append(a_)
        d_pt.append(d_)
    rec = sb("rec", [128, 4], F32)
    d_rec = Dep("rec")
    units = [act_view(2 * u, 2, F32, 512) for u in range(8)]
    xcs = [units[0][0], units[1][0]]
    d_xcs = [units[0][1], units[1][1]]
    zb = [units[2][0], units[3][0]]
    d_zb = [units[2][1], units[3][1]]
    ub = [sb(f"ub{i}", [128, 514], F32) for i in range(2)]
    d_ub = [Dep(f"ub{i}") for i in range(2)]
    sa = [units[0][0], units[1][0]]
    d_sa = [units[0][1], units[1][1]]
    sbb = [units[2][0], units[3][0]]
    d_sbb = [units[2][1], units[3][1]]
    m1 = [units[4][0], units[5][0]]
    d_m1 = [units[4][1], units[5][1]]
    m2 = [units[6][0], units[7][0]]
    d_m2 = [units[6][1], units[7][1]]
    tA, d_tA = units[0]
    tB, d_tB = units[1]
    tC, d_tC = units[2]
    tI, d_tI = units[3]
    tA = tA.rearrange("p (t f) -> p t f", f=16)
    tB = tB.rearrange("p (t f) -> p t f", f=16)
    tC = tC.rearrange("p (t f) -> p t f", f=16)
    tI = tI.bitcast(I32).rearrange("p (t f) -> p t f", f=16)
    posf = sb("posf", [128, NTILE], F32)
    d_posf = Dep("posf")

    banks = [nc.alloc_psum_tensor(f"bank{i}", [128, 512], F32) for i in range(8)]
    d_bank = [Dep(f"bank{i}", excl=True) for i in range(8)]
    rot = {"i": 0, "n": 6}
    pending = [False] * 8
    bank_idx = {}

    def psum(hold=False):
        for _ in range(rot["n"]):
            i = rot["i"] % rot["n"]
            rot["i"] += 1
            if not pending[i]:
                break
        assert not pending[i], "no free PSUM bank"
        pending[i] = hold
        return banks[i], d_bank[i]

    def prelease(dep):
        pending[d_bank.index(dep)] = False

    def bview(bank):
        return bank[:].bitcast(BF16).rearrange("p (a b) -> p a b", a=8)

    s_set = fw.new_dma_sem("set")
    s_xin = [fw.new_dma_sem(f"xin{i}") for i in range(2)]
    s_xout = [fw.new_dma_sem(f"xout{i}") for i in range(2)]

    npieces = len(plan) * ng
    wstate = {"next_issue": 0, "next_use": 0, "free": list(range(NSLOT)), "slot_of": {}, "held": {}}

    def w_fill():
        while wstate["free"] and wstate["next_issue"] < npieces:
            i = wstate["next_issue"]
            wstate["next_issue"] += 1
            n, F = plan[i % len(plan)]
            off = offs[i % len(plan)]
            s_ = wstate["free"].pop(0)
            wstate["slot_of"][i] = s_
            fw.dma("pool", s_w[s_],
                   [lambda e, s_=s_, off=off, F=F: e.dma_start(out=wslot[s_][:, 0:F], in_=ws_d[:, off:off + F])],
                   writes=[d_w[s_]])

    def w_get(name, keep=()):
        i = wstate["next_use"]
        wstate["next_use"] += 1
        n, F = plan[i % len(plan)]
        assert n == name, (n, name)
        for hn in list(wstate["held"].keys()):
            if hn not in keep:
                hi = wstate["held"].pop(hn)
                wstate["free"].append(wstate["slot_of"].pop(hi))
        w_fill()
        assert i in wstate["slot_of"], ("weight piece not issued (no free slot)", name)
        wstate["held"][name] = i
        s_ = wstate["slot_of"][i]
        return wslot[s_], d_w[s_], F

    def rsqrt_newton(dst, d_dst, src, d_src, n, mul, eps=EPS):
        ra = n_ra[:, 0:n]
        op("act", lambda e: e.activation(out=ra, in_=src, func=AF.Ln, scale=mul, bias=epsT[:]),
           reads=[d_src, d_epsT], writes=[d_nra])
        op("act", lambda e: e.activation(out=dst, in_=ra, func=AF.Exp, scale=-0.5), reads=[d_nra], writes=[d_dst])

    d_ssc = [Dep(f"ssc{i}") for i in range(TPG)]
    d_rac = [Dep(f"rac{i}") for i in range(TPG)]
    d_rrc = [Dep(f"rrc{i}") for i in range(TPG)]

    def norm_begin():
        op("dve", lambda e: e.memset(ss[:, 0:TPG], 0.0), writes=d_ssc)

    def norm_stats(xb, tt):
        c1 = slice(tt, tt + 1)
        op("act", lambda e: e.activation(out=junk[:], in_=xres[xb][:, tt, :], func=AF.Square, accum_out=ss[:, c1]),
           reads=[d_x[xb][tt]], writes=[d_junk, d_ssc[tt]])
        op("act", lambda e: e.activation(out=n_rb[:, c1], in_=ss[:, c1], func=AF.Ln, scale=1.0 / D, bias=epsT[:]),
           reads=[d_ssc[tt], d_epsT], writes=[d_rac[tt]])
        op("act", lambda e: e.activation(out=rr[:, c1], in_=n_rb[:, c1], func=AF.Exp, scale=-0.5),
           reads=[d_rac[tt]], writes=[d_rrc[tt]])

    def norm_apply(xb, gcol, tt):
        b = tt % 2
        c1 = slice(tt, tt + 1)
        op("act", lambda e: e.activation(out=xn[b][:], in_=xres[xb][:, tt, :], func=AF.Copy, scale=rr[:, c1]),
           reads=[d_x[xb][tt], d_rrc[tt]], writes=[d_xn[b]])
        tb_, d_tb = banks[6 + b], d_bank[6 + b]
        tv = bview(tb_)
        for k in range(8):
            op("pe", lambda e, k=k: e.transpose(out=tv[:, k, :], in_=xn[b][:, k * 128:(k + 1) * 128], identity=identb[:]),
               reads=[d_xn[b], d_id], writes=[d_tb], signal=(k == 7))
        op("dve", lambda e: e.tensor_tensor(
            out=hT[:, :, tt * 128:(tt + 1) * 128], in0=tv,
            in1=par[:, gcol:gcol + 8, None].broadcast_to([128, 8, 128]), op=ALU.mult),
           reads=[d_tb, d_par], writes=[d_hT[tt]])

    def norm_tail_step(xb, gcol, tt, tail_hook=None):
        if tt == TPG - 1:
            for t2 in range(max(0, tt - NDELAY), tt):
                norm_apply(xb, gcol, t2)
            if tail_hook is not None:
                tail_hook()
            norm_stats(xb, tt)
            norm_apply(xb, gcol, tt)
        else:
            if tt >= NDELAY:
                norm_apply(xb, gcol, tt - NDELAY)
            norm_stats(xb, tt)

    def norm_to_hT(xb, gcol):
        norm_begin()
        for tt in range(TPG):
            norm_stats(xb, tt)
            norm_apply(xb, gcol, tt)

    def ffn(xb, f, gcol, do_norm, same_norm=None, hoist_norm=None, tail_hook=None):
        if do_norm:
            norm_to_hT(xb, gcol)
        checkpoint("norm")
        for c in range(NFC):
            wt, dw, F = w_get(f"{f}gu{c}")
            wv = wt[:, 0:F].rearrange("p (a k m) -> p a k m", a=2, k=8)
            bg, dbg = psum()
            bu, dbu = psum()
            for k in range(8):
                op("pe", lambda e, k=k, wv=wv, bg=bg: e.matmul(bg[:], lhsT=wv[:, 0, k, :], rhs=hT[:, k, :],
                                                               start=(k == 0), stop=(k == 7)),
                   reads=[dw] + d_hT, writes=[dbg], signal=(k == 7))
            for k in range(8):
                op("pe", lambda e, k=k, wv=wv, bu=bu: e.matmul(bu[:], lhsT=wv[:, 1, k, :], rhs=hT[:, k, :],
                                                               start=(k == 0), stop=(k == 7)),
                   reads=[dw] + d_hT, writes=[dbu], signal=(k == 7))
            b = c % 2
            op("act", lambda e, b=b, bg=bg: e.activation(out=sg[b][:], in_=bg[:], func=AF.Silu),
               reads=[dbg], writes=[d_sg[b]])
            op("dve", lambda e, b=b, bu=bu, c=c: e.tensor_tensor(out=actT[:, c, :], in0=sg[b][:], in1=bu[:], op=ALU.mult),
               reads=[d_sg[b], dbu], writes=[d_act[c]])
        checkpoint("ffnup")
        if hoist_norm is not None:
            norm_begin()
            for tt in range(TPG):
                norm_stats(hoist_norm[0], tt)
        if same_norm is not None:
            norm_begin()

        def evac(tt, hf, ab, dab):
            op("dve", lambda e: e.scalar_tensor_tensor(
                out=xres[xb][:, tt, hf * 512:(hf + 1) * 512], in0=ab[:], scalar=0.5,
                in1=xres[xb][:, tt, hf * 512:(hf + 1) * 512], op0=ALU.mult, op1=ALU.add),
               reads=[dab, d_x[xb][tt]], writes=[d_x[xb][tt]])

        bi = 0
        for hf in range(2):
            accs = [psum() for _ in range(TPG)]
            for kh, (k0, k1) in enumerate(KR):
                wt, dw, F = w_get(f"{f}d{hf}{kh}")
                wv = wt[:, 0:F].rearrange("p (k n) -> p k n", k=k1 - k0)
                final = (hf == 1 and kh == len(KR) - 1)
                for tt in range(TPG):
                    ab, dab = accs[tt]
                    for kc in range(k0, k1):
                        op("pe", lambda e, kc=kc, k0=k0, tt=tt, wv=wv, ab=ab: e.matmul(
                            ab[:], lhsT=actT[:, kc, tt * 128:(tt + 1) * 128], rhs=wv[:, kc - k0, :],
                            start=(kc == 0), stop=(kc == NFC - 1)),
                           reads=[dw, d_act[kc]], writes=[dab], signal=(kc == k1 - 1))
                    if final:
                        evac(tt, hf, ab, dab)
                        if same_norm is not None:
                            norm_tail_step(xb, same_norm, tt, tail_hook)
                if hoist_norm is not None and bi in (1, 2, 4, 5):
                    norm_apply(hoist_norm[0], hoist_norm[1], {1: 0, 2: 1, 4: 2, 5: 3}[bi])
                bi += 1
            if hf == 0:
                for tt in range(TPG):
                    evac(tt, hf, *accs[tt])
    def rope(dst1, dst2, t1, t2, cosb, sinb, tmp, d_tmp, reads, writes, eng="dve"):
        a, b_, c_, d_ = tmp
        da, db, dc, dd = d_tmp
        op(eng, lambda e: e.tensor_tensor(out=a, in0=t1, in1=cosb, op=ALU.mult), reads=reads, writes=[da])
        op(eng, lambda e: e.tensor_tensor(out=b_, in0=t2, in1=sinb, op=ALU.mult), reads=reads, writes=[db])
        op(eng, lambda e: e.tensor_tensor(out=c_, in0=t1, in1=sinb, op=ALU.mult), reads=reads, writes=[dc])
        op(eng, lambda e: e.tensor_tensor(out=d_, in0=t2, in1=cosb, op=ALU.mult), reads=reads, writes=[dd])
        op(eng, lambda e: e.tensor_tensor(out=dst1, in0=a, in1=b_, op=ALU.subtract), reads=[da, db], writes=writes)
        op(eng, lambda e: e.tensor_tensor(out=dst2, in0=c_, in1=d_, op=ALU.add), reads=[dc, dd], writes=writes)

    def mixer(xb, g, same_norm=None):
        gi = g % GPS
        tile0 = g * TPG
        wl, dwl, F = w_get("lat", keep=("f1d12",))
        wlv = wl[:, 0:F].rearrange("p (k n) -> p k n", k=8)
        wqk, dwq, F = w_get("uqkv", keep=("lat",))
        dwk = dwq
        wqv = wqk[:, 0:2304].rearrange("p (k n) -> p k n", k=3)
        wkv = wqk[:, 2304:4352].rearrange("p (k n) -> p k n", k=2)
        T_ = {}

        def early1(tt):
            b = tt % 2
            tsl = slice(tt * 128, (tt + 1) * 128)
            pA, dpA = psum()
            pB, dpB = psum()
            for k in range(8):
                op("pe", lambda e, k=k: e.matmul(pA[:, 0:384], lhsT=hT[:, k, tsl], rhs=wlv[:, k, 0:384],
                                                 start=(k == 0), stop=(k == 7)),
                   reads=[dwl, d_hT[tt]], writes=[dpA], signal=(k == 7))
            for k in range(8):
                op("pe", lambda e, k=k: e.matmul(pB[:, 0:288], lhsT=hT[:, k, tsl], rhs=wlv[:, k, 384:672],
                                                 start=(k == 0), stop=(k == 7)),
                   reads=[dwl, d_hT[tt]], writes=[dpB], signal=(k == 7))
            op("dve", lambda e: e.memset(ssl2[b][:], 0.0), writes=[d_ssl2[b]])
            op("act", lambda e: e.activation(out=junk[:, 0:384], in_=pA[:, 0:384], func=AF.Square, scale=math.sqrt(float(KVL) / QL),
                                              accum_out=ssl2[b][:, 0:1]),
               reads=[dpA], writes=[d_junk, d_ssl2[b]])
            op("act", lambda e: e.activation(out=qlb[:], in_=pA[:, 0:384], func=AF.Copy), reads=[dpA], writes=[d_qlb])
            op("act", lambda e: e.activation(out=junk[:, 0:256], in_=pB[:, 0:256], func=AF.Square, accum_out=ssl2[b][:, 1:2]),
               reads=[dpB], writes=[d_junk, d_ssl2[b]])
            op("dve", lambda e: e.tensor_copy(out=kvb[:], in_=pB[:, 0:256]), reads=[dpB], writes=[d_kvb])
            op("dve", lambda e: e.tensor_copy(out=krf2[b][:], in_=pB[:, 256:288]), reads=[dpB], writes=[d_krf2[b]])
            rsqrt_newton(rl2[b][:, 0:2], d_rl2[b], ssl2[b][:, 0:2], d_ssl2[b], 2, 1.0 / KVL)
            op("act", lambda e: e.activation(out=SQ[:, 8:16, 64:96], in_=krf2[b][:, None, :].broadcast_to([128, NH, 32]),
                                              func=AF.Square),
               reads=[d_krf2[b]], writes=[d_SQ])

        def early2(tt):
            tb_, d_tb = psum()
            tv = bview(tb_)
            for j in range(3):
                op("pe", lambda e, j=j: e.transpose(out=tv[:, j, :], in_=qlb[:, j * 128:(j + 1) * 128], identity=identb[:]),
                   reads=[d_qlb, d_id], writes=[d_tb], signal=False)
            for j in range(2):
                op("pe", lambda e, j=j: e.transpose(out=tv[:, 3 + j, :], in_=kvb[:, j * 128:(j + 1) * 128], identity=identb[:]),
                   reads=[d_kvb, d_id], writes=[d_tb], signal=(j == 1))
            op("dve", lambda e: e.tensor_tensor(out=lt5[:], in0=tv[:, 0:5, :],
                                                 in1=par[:, P_GQ:P_GQ + 5, None].broadcast_to([128, 5, 128]), op=ALU.mult),
               reads=[d_tb, d_par], writes=[d_lt5])
            pq0, dpq0 = psum(hold=True)
            pq1, dpq1 = psum(hold=True)
            pk, dpk = psum(hold=True)
            pv, dpv = psum(hold=True)
            for j in range(3):
                op("pe", lambda e, j=j: e.matmul(pq0[:, 0:384], lhsT=lt5[:, j, :], rhs=wqv[:, j, 0:384],
                                                 start=(j == 0), stop=(j == 2)),
                   reads=[dwq, d_lt5], writes=[dpq0], signal=(j == 2))
            for j in range(3):
                op("pe", lambda e, j=j: e.matmul(pq1[:, 0:384], lhsT=lt5[:, j, :], rhs=wqv[:, j, 384:768],
                                                 start=(j == 0), stop=(j == 2)),
                   reads=[dwq, d_lt5], writes=[dpq1], signal=(j == 2))
            for j in range(2):
                op("pe", lambda e, j=j: e.matmul(pk[:], lhsT=lt5[:, 3 + j, :], rhs=wkv[:, j, 0:512],
                                                 start=(j == 0), stop=(j == 1)),
                   reads=[dwk, d_lt5], writes=[dpk], signal=(j == 1))
            for j in range(2):
                op("pe", lambda e, j=j: e.matmul(pv[:], lhsT=lt5[:, 3 + j, :], rhs=wkv[:, j, 512:1024],
                                                 start=(j == 0), stop=(j == 1)),
                   reads=[dwk, d_lt5], writes=[dpv], signal=(j == 1))
            T_[tt] = (pq0, dpq0, pq1, dpq1, pk, dpk, pv, dpv)

        def late6(tt):
            b = tt % 2
            st = gi * TPG + tt
            pq0, dpq0, pq1, dpq1, pk, dpk, pv, dpv = T_.pop(tt)
            for d_ in (dpq0, dpq1, dpk, dpv):
                prelease(d_)
            rl_, d_rl_ = rl2[b], d_rl2[b]
            op("act", lambda e: e.activation(out=SQ[:, 0:4, :], in_=pq0[:, 0:384].rearrange("p (h d) -> p h d", h=4),
                                              func=AF.Square, scale=rl_[:, 0:1]),
               reads=[dpq0, d_rl_], writes=[d_SQ])
            op("act", lambda e: e.activation(out=SQ[:, 4:8, :], in_=pq1[:, 0:384].rearrange("p (h d) -> p h d", h=4),
                                              func=AF.Square, scale=rl_[:, 0:1]),
               reads=[dpq1, d_rl_], writes=[d_SQ])
            op("act", lambda e: e.activation(out=SQ[:, 8:16, 0:64], in_=pk[:].rearrange("p (h d) -> p h d", h=NH),
                                              func=AF.Square, scale=rl_[:, 1:2]),
               reads=[dpk, d_rl_], writes=[d_SQ])
            op("dve", lambda e: e.tensor_reduce(out=ssh[:, 0:16], in_=SQ[:], axis=AX.X, op=ALU.add), reads=[d_SQ], writes=[d_sshq, d_sshk])
            op("act", lambda e: e.activation(out=qs[:, 0:384], in_=pq0[:, 0:384], func=AF.Copy, scale=rl_[:, 0:1]),
               reads=[dpq0, d_rl_], writes=[d_qs])
            op("act", lambda e: e.activation(out=qs[:, 384:768], in_=pq1[:, 0:384], func=AF.Copy, scale=rl_[:, 0:1]),
               reads=[dpq1, d_rl_], writes=[d_qs])
            op("act", lambda e: e.activation(out=ks[:], in_=pk[:], func=AF.Copy, scale=rl_[:, 1:2]),
               reads=[dpk, d_rl_], writes=[d_ks])
            op("dve", lambda e: e.tensor_scalar(out=Vc[:, st, :, 0:64], in0=pv[:].rearrange("p (h d) -> p h d", h=NH),
                                                 scalar1=rl_[:, 1:2], scalar2=None, op0=ALU.mult),
               reads=[dpv, d_rl_], writes=[d_V[st]])

        def late78(tt):
            b = tt % 2
            rsqrt_newton(rh[:, 0:16], d_rh, ssh[:, 0:16], [d_sshq, d_sshk], 16, 1.0 / DQK)

        def late9(tt):
            b = tt % 2
            gt_ = tile0 + tt
            qsv = qs[:].rearrange("p (h d) -> p h d", h=NH)
            ksv = ks[:].rearrange("p (h d) -> p h d", h=NH)
            op("dve", lambda e: e.tensor_tensor(out=qr[:], in0=qsv[:, :, 64:96], in1=rh[:, 0:8, None].broadcast_to([128, NH, 32]),
                                                 op=ALU.mult), reads=[d_qs, d_rh], writes=[d_qr])
            op("dve", lambda e: e.tensor_tensor(out=qr[:], in0=qr[:],
                                                 in1=par[:, None, P_GQH + 64:P_GQH + 96].broadcast_to([128, NH, 32]), op=ALU.mult),
               reads=[d_qr, d_par], writes=[d_qr])
            cosb = cosT[:, gt_, None, :].broadcast_to([128, NH, 16])
            sinb = sinT[:, gt_, None, :].broadcast_to([128, NH, 16])
            rope(qf[:, :, 64:80], qf[:, :, 80:96], qr[:, :, 0:16], qr[:, :, 16:32], cosb, sinb,
                 [t[:] for t in rt], d_rt, [d_qr, d_cos, d_sin], [d_qf])
            op("dve", lambda e: e.tensor_tensor(out=qf[:, :, 0:64], in0=qsv[:, :, 0:64], in1=rh[:, 0:8, None].broadcast_to([128, NH, 64]),
                                                 op=ALU.mult), reads=[d_qs, d_rh], writes=[d_qf])
            op("dve", lambda e: e.tensor_tensor(out=kf[:, :, 0:64], in0=ksv, in1=rh[:, 8:16, None].broadcast_to([128, NH, 64]),
                                                 op=ALU.mult), reads=[d_ks, d_rh], writes=[d_kf])
            op("dve", lambda e: e.tensor_tensor(out=kb[:], in0=krf2[b][:], in1=par[:, P_GKH + 64:P_GKH + 96], op=ALU.mult),
               reads=[d_krf2[b], d_par], writes=[d_kb])
            rope(kbr[:, 0:16], kbr[:, 16:32], kb[:, 0:16], kb[:, 16:32], cosT[:, gt_, :], sinT[:, gt_, :],
                 [t[:] for t in kt4], d_kt4, [d_kb, d_cos, d_sin], [d_kbr])
            op("dve", lambda e: e.tensor_tensor(out=kf[:, :, 64:96], in0=kbr[:, None, :].broadcast_to([128, NH, 32]),
                                                 in1=rh[:, 8:16, None].broadcast_to([128, NH, 32]), op=ALU.mult),
               reads=[d_kbr, d_rh], writes=[d_kf])

        def late10(tt):
            st = gi * TPG + tt
            tsl = slice(tt * 128, (tt + 1) * 128)
            tq, d_tq = psum()
            tk, d_tk = psum()
            tqv, tkv = bview(tq), bview(tk)
            for h in range(NH):
                op("pe", lambda e, h=h: e.transpose(out=tqv[0:96, h, :], in_=qf[:, h, :], identity=identb[:]),
                   reads=[d_qf, d_id], writes=[d_tq], signal=(h == NH - 1))
            for h in range(NH):
                op("pe", lambda e, h=h: e.transpose(out=tkv[0:96, h, :], in_=kf[:, h, :], identity=identb[:]),
                   reads=[d_kf, d_id], writes=[d_tk], signal=(h == NH - 1))
            op("act", lambda e: e.activation(out=QT[0:96, :, tsl], in_=tqv[0:96, :, :], func=AF.Copy,
                                              scale=par[0:96, P_GQC:P_GQC + 1]),
               reads=[d_tq, d_par], writes=[d_QT[tt]])
            op("dve", lambda e: e.tensor_scalar(out=KT[0:96, :, st * 128:(st + 1) * 128], in0=tkv[0:96, :, :],
                                                 scalar1=par[0:96, P_GKC:P_GKC + 1], scalar2=None, op0=ALU.mult),
               reads=[d_tk, d_par], writes=[d_KT[st]])


        CV = {}

        def conv_pe(c, keep):
            wt, dw, F = w_get(f"conv{c}", keep=keep)
            wv = wt[:, 0:F].rearrange("p (k i m) -> p k i m", k=8, i=3)
            bks = [psum(hold=True), psum(hold=True), psum(hold=True)]
            for i_, (pb_, dpb_) in enumerate(bks):
                for k in range(8):
                    op("pe", lambda e, k=k, i_=i_, pb_=pb_, wv=wv: e.matmul(pb_[:], lhsT=wv[:, k, i_, :], rhs=hT[:, k, :],
                                                                           start=(k == 0), stop=(k == 7)),
                       reads=[dw] + d_hT, writes=[dpb_], signal=(k == 7))
            CV[c] = bks

        def conv_rest(c):
            (pxc, dpxc), (pgb, dpgb), (pgc, dpgc) = CV.pop(c)
            for d_ in (dpxc, dpgb, dpgc):
                prelease(d_)
            b = c % 2
            op("act", lambda e: e.activation(out=xcs[b][:], in_=pxc[:], func=AF.Copy), reads=[dpxc], writes=[d_xcs[b]])
            if gi == 0:
                op("dve", lambda e: e.memset(ub[b][:, 0:2], 0.0), writes=[d_ub[b]])
            else:
                op("dve", lambda e: e.tensor_copy(out=ub[b][:, 0:2], in_=carry[:, c, :]), reads=[d_carry[c]], writes=[d_ub[b]])
            op("dve", lambda e: e.tensor_tensor(out=ub[b][:, 2:514], in0=pgc[:], in1=xcs[b][:], op=ALU.mult),
               reads=[dpgc, d_xcs[b]], writes=[d_ub[b]])
            op("dve", lambda e: e.tensor_copy(out=carry[:, c, :], in_=ub[b][:, 512:514]), reads=[d_ub[b]], writes=[d_carry[c]])
            cw0 = P_CW + c * 3
            op("dve", lambda e: e.tensor_scalar(out=zb[b][:], in0=ub[b][:, 0:512], scalar1=par[:, cw0:cw0 + 1],
                                                 scalar2=None, op0=ALU.mult),
               reads=[d_ub[b], d_par], writes=[d_zb[b]])
            for k_ in (1, 2):
                op("dve", lambda e, k_=k_: e.scalar_tensor_tensor(
                    out=zb[b][:], in0=ub[b][:, k_:k_ + 512], scalar=par[:, cw0 + k_:cw0 + k_ + 1], in1=zb[b][:],
                    op0=ALU.mult, op1=ALU.add),
                   reads=[d_ub[b], d_par, d_zb[b]], writes=[d_zb[b]])
            op("dve", lambda e: e.tensor_tensor(out=convT[:, c, :], in0=pgb[:], in1=zb[b][:], op=ALU.mult),
               reads=[dpgb, d_zb[b]], writes=[d_conv[c]])

        rot["n"] = 8
        rot["i"] = 0
        early1(0)
        yield
        early2(0)
        for tt in range(TPG):
            last = (tt == TPG - 1)
            late6(tt)
            late78(tt)
            if not last:
                early1(tt + 1)
            else:
                conv_pe(0, ())
            if not last:
                early2(tt + 1)
            else:
                conv_pe(1, ())
            late9(tt)
            late10(tt)
        conv_rest(0)
        conv_rest(1)
        rot["n"] = 6
        for c in range(2, 8):
            conv_pe(c, ())
            conv_rest(c)
        checkpoint("m1")

        qb = gi
        nkt = 4 * qb + 4
        sc = 1.0 / math.sqrt(DQK)
        steps = [(h, kt) for h in range(NH) for kt in range(nkt)]
        st_state = {}

        def emit_S(si):
            h, kt = steps[si]
            jmin = max(0, kt - 4 * qb)
            c0 = jmin * 128
            n = 512 - c0
            ps, dps = psum()
            op("pe", lambda e, ps=ps, h=h, kt=kt, c0=c0, n=n: e.matmul(
                ps[:, 0:n], lhsT=KT[0:96, h, kt * 128:(kt + 1) * 128], rhs=QT[0:96, h, c0:512], start=True, stop=True),
               reads=[d_KT[kt]] + d_QT[jmin:], writes=[dps])
            pb = si % 4
            op("act", lambda e, ps=ps, pb=pb, n=n: e.activation(out=pt[pb][:, 0:n], in_=ps[:, 0:n], func=AF.Exp,
                                                                 scale=sc, bias=negM[:]),
               reads=[dps, d_negM], writes=[d_pt[pb]])
            if kt >= 4 * qb:
                op("dve", lambda e, pb=pb: e.tensor_tensor(out=pt[pb][:, 0:128], in0=pt[pb][:, 0:128], in1=trib[:], op=ALU.mult),
                   reads=[d_pt[pb], d_tri], writes=[d_pt[pb]])
            st_state[si] = (pb, jmin)

        LOOK = 3
        nxt = 0
        for si, (h, kt) in enumerate(steps):
            while nxt < len(steps) and nxt <= si + LOOK:
                emit_S(nxt)
                nxt += 1
            pb, jmin = st_state.pop(si)
            Ob, dOb = banks[6 + (h % 2)], d_bank[6 + (h % 2)]
            Ov = Ob[:, 0:260].rearrange("p (j d) -> p j d", j=4)
            for j in range(jmin, 4):
                last = (kt == 4 * qb + j)
                op("pe", lambda e, j=j, jmin=jmin, pb=pb, kt=kt, h=h, Ov=Ov, last=last: e.matmul(
                    Ov[:, j, :], lhsT=pt[pb][:, (j - jmin) * 128:(j - jmin + 1) * 128], rhs=Vc[:, kt, h, :],
                    start=(kt == 0 and j == 0), stop=last, skip_group_check=True),
                   reads=[d_pt[pb], d_V[kt]], writes=[dOb], signal=(j == 3))
            if kt == nkt - 1:
                op("dve", lambda e, Ov=Ov: e.reciprocal(out=rec[:], in_=Ov[:, :, 64]), reads=[dOb], writes=[d_rec])
                op("dve", lambda e, Ov=Ov, h=h: e.tensor_tensor(out=attn_tok[:, :, h * 64:(h + 1) * 64], in0=Ov[:, :, 0:64],
                                                                 in1=rec[:, :, None].broadcast_to([128, 4, 64]), op=ALU.mult),
                   reads=[dOb, d_rec], writes=d_atok)
        for j in range(TPG):
            tb_, d_tb = banks[6 + (j % 2)], d_bank[6 + (j % 2)]
            tv = bview(tb_)
            for fc in range(4):
                op("pe", lambda e, j=j, fc=fc, tv=tv: e.transpose(out=tv[:, fc, :], in_=attn_tok[:, j, fc * 128:(fc + 1) * 128],
                                                                 identity=identb[:]),
                   reads=[d_atok[j], d_id], writes=[d_tb], signal=(fc == 3))
            op("act", lambda e, j=j, tv=tv: e.activation(out=attnT[:, :, j * 128:(j + 1) * 128], in_=tv[:, 0:4, :], func=AF.Copy),
               reads=[d_tb], writes=[d_attnT[j]])

        checkpoint("m2")
        if dbg and g == 0:
            fw.dma("sp", s_dbg, [lambda e: e.dma_start(out=dbg_attn[:, :, :], in_=attn_tok[:]),
                                 lambda e: e.dma_start(out=dbg_q[:, :, :], in_=QT[0:96, :, :]),
                                 lambda e: e.dma_start(out=dbg_k[:, :, :], in_=KT[0:96, :, 0:G]),
                                 lambda e: e.dma_start(out=dbg_v[:, :, :, :], in_=Vc[:, 0:TPG, :, :])],
                   reads=[d_atok, d_QT, d_KT[0:TPG], d_V[0:TPG]])
        checkpoint("m3")
        for c in range(8):
            wt, dw, F = w_get(f"m4{c}")
            wv = wt[:, 0:F].rearrange("p (k m) -> p k m", k=28)
            pyb, dpyb = psum()
            pya, dpya = psum()
            pga, dpga = psum()
            pgb, dpgb = psum()
            for k in range(8):
                op("pe", lambda e, k=k, wv=wv, pyb=pyb: e.matmul(pyb[:], lhsT=wv[:, k, :], rhs=convT[:, k, :],
                                                                 start=(k == 0), stop=(k == 7)),
                   reads=[dw, d_conv[k]], writes=[dpyb], signal=(k == 7))
            for k in range(4):
                op("pe", lambda e, k=k, wv=wv, pya=pya: e.matmul(pya[:], lhsT=wv[:, 8 + k, :], rhs=attnT[:, k, :],
                                                                 start=(k == 0), stop=(k == 3)),
                   reads=[dw] + d_attnT, writes=[dpya], signal=(k == 3))
            for k in range(8):
                op("pe", lambda e, k=k, wv=wv, pga=pga: e.matmul(pga[:], lhsT=wv[:, 12 + k, :], rhs=hT[:, k, :],
                                                                 start=(k == 0), stop=(k == 7)),
                   reads=[dw] + d_hT, writes=[dpga], signal=(k == 7))
            for k in range(8):
                op("pe", lambda e, k=k, wv=wv, pgb=pgb: e.matmul(pgb[:], lhsT=wv[:, 20 + k, :], rhs=hT[:, k, :],
                                                                 start=(k == 0), stop=(k == 7)),
                   reads=[dw] + d_hT, writes=[dpgb], signal=(k == 7))
            b = c % 2
            op("act", lambda e, b=b, c=c, pga=pga: e.activation(out=sa[b][:], in_=pga[:], func=AF.Sigmoid,
                                                                 bias=par[:, P_GB + c:P_GB + c + 1]),
               reads=[dpga, d_par], writes=[d_sa[b]])
            op("act", lambda e, b=b, c=c, pgb=pgb: e.activation(out=sbb[b][:], in_=pgb[:], func=AF.Sigmoid,
                                                                 bias=par[:, P_GB + 8 + c:P_GB + 9 + c]),
               reads=[dpgb, d_par], writes=[d_sbb[b]])
            op("dve", lambda e, b=b, pya=pya: e.tensor_tensor(out=m1[b][:], in0=pya[:], in1=sa[b][:], op=ALU.mult),
               reads=[dpya, d_sa[b]], writes=[d_m1[b]])
            op("dve", lambda e, b=b, pyb=pyb: e.tensor_tensor(out=m2[b][:], in0=pyb[:], in1=sbb[b][:], op=ALU.mult),
               reads=[dpyb, d_sbb[b]], writes=[d_m2[b]])
            op("dve", lambda e, b=b, c=c: e.tensor_tensor(out=mrgT[:, c, :], in0=m1[b][:], in1=m2[b][:], op=ALU.add),
               reads=[d_m1[b], d_m2[b]], writes=[d_mrg[c]])

        checkpoint("m4")
        wo_ = []
        for hf in range(2):
            wt, dw, F = w_get(f"wo{hf}", keep=("wo0",))
            wo_.append((wt[:, 0:F].rearrange("p (k n) -> p k n", k=8), dw))
        if same_norm is not None:
            norm_begin()
        for tt in range(TPG):
            for hf in range(2):
                wv, dw = wo_[hf]
                po, dpo = psum()
                for k in range(8):
                    op("pe", lambda e, k=k, tt=tt, wv=wv, po=po: e.matmul(po[:], lhsT=mrgT[:, k, tt * 128:(tt + 1) * 128],
                                                                         rhs=wv[:, k, :], start=(k == 0), stop=(k == 7)),
                       reads=[dw, d_mrg[k]], writes=[dpo], signal=(k == 7))
                op("dve", lambda e, tt=tt, hf=hf, po=po: e.tensor_tensor(
                    out=xres[xb][:, tt, hf * 512:(hf + 1) * 512], in0=po[:], in1=xres[xb][:, tt, hf * 512:(hf + 1) * 512],
                    op=ALU.add),
                   reads=[dpo, d_x[xb][tt]], writes=[d_x[xb][tt]])
            if same_norm is not None:
                norm_tail_step(xb, same_norm, tt)

    def load_x(g):
        xb = g % 2
        src = x_d[g * G:(g + 1) * G, :].rearrange("(t p) d -> p t d", p=128)
        fw.dma("sp", s_xin[xb], [lambda e, xb=xb, src=src: e.dma_start(out=xres[xb][:], in_=src)], writes=d_x[xb])

    def store_x(g):
        xb = g % 2
        dst = out_d[g * G:(g + 1) * G, :].rearrange("(t p) d -> p t d", p=128)
        fw.dma("sp", s_xout[xb], [lambda e, xb=xb, dst=dst: e.dma_start(out=dst, in_=xres[xb][:])], reads=d_x[xb])

    fw.dma("sp", s_set, [lambda e: e.dma_start(out=par[:], in_=par_d[:, :]),
                         lambda e: e.dma_start(out=pos_t[:], in_=pos_d[:, :])], writes=[d_par, d_pos])
    load_x(0)
    op("pool", lambda e: e.memset(Vc[:], 1.0), writes=d_V)
    op("dve", lambda e: e.tensor_copy(out=identb[:], in_=par[:, P_ID:P_ID + 128]), reads=[d_par], writes=[d_id])
    op("dve", lambda e: e.tensor_copy(out=trib[:], in_=par[:, P_TRI:P_TRI + 128]), reads=[d_par], writes=[d_tri])
    op("dve", lambda e: e.memset(negM[:], -SM_SHIFT), writes=[d_negM])
    op("dve", lambda e: e.memset(epsT[:], EPS), writes=[d_epsT])
    op("dve", lambda e: e.tensor_copy(out=posf[:], in_=pos_t[:]), reads=[d_pos], writes=[d_posf])
    op("dve", lambda e: e.tensor_tensor(out=tA[:], in0=posf[:, :, None].broadcast_to([128, NTILE, 16]),
                                         in1=par[:, None, P_INVF:P_INVF + 16].broadcast_to([128, NTILE, 16]), op=ALU.mult),
       reads=[d_posf, d_par], writes=[d_tA])
    TWO_PI = 2.0 * math.pi
    C1 = 6.28125
    C2 = TWO_PI - C1
    for (shift, dst, d_dst) in ((0.0, sinT, d_sin), (0.5 * math.pi, cosT, d_cos)):
        op("dve", lambda e, shift=shift: e.tensor_scalar(out=tB[:], in0=tA[:], scalar1=shift, scalar2=None, op0=ALU.add),
           reads=[d_tA], writes=[d_tB])
        op("dve", lambda e: e.tensor_scalar(out=tI[:], in0=tB[:], scalar1=1.0 / TWO_PI, scalar2=None, op0=ALU.mult),
           reads=[d_tB], writes=[d_tI])
        op("dve", lambda e: e.tensor_copy(out=tC[:], in_=tI[:]), reads=[d_tI], writes=[d_tC])
        op("dve", lambda e: e.scalar_tensor_tensor(out=tB[:], in0=tC[:], scalar=-C1, in1=tB[:], op0=ALU.mult, op1=ALU.add),
           reads=[d_tC, d_tB], writes=[d_tB])
        op("dve", lambda e: e.scalar_tensor_tensor(out=tB[:], in0=tC[:], scalar=-C2, in1=tB[:], op0=ALU.mult, op1=ALU.add),
           reads=[d_tC, d_tB], writes=[d_tB])
        op("dve", lambda e: e.tensor_scalar(out=tB[:], in0=tB[:], scalar1=-3.1415925, scalar2=3.1415925, op0=ALU.max, op1=ALU.min),
           reads=[d_tB], writes=[d_tB])
        op("act", lambda e, dst=dst: e.activation(out=dst[:], in_=tB[:], func=AF.Sin), reads=[d_tB], writes=[d_dst])

    s_dbg = fw.new_dma_sem("dbg")

    def dump(i, xb):
        dst = dbg_d[i].rearrange("(t p) d -> p t d", p=128)
        fw.dma("sp", s_dbg, [lambda e, xb=xb, dst=dst: e.dma_start(out=dst, in_=xres[xb][:])], reads=d_x[xb])

    try:
        checkpoint("setup")
        for g in range(ng):
            xb = g % 2
            mx = mixer(xb, g, same_norm=P_G2)
            ffn(xb, "f1", P_G1, do_norm=(g == 0), same_norm=P_GM, tail_hook=lambda mx=mx: next(mx))
            checkpoint("ffn1")
            if dbg and g == 0:
                dump(0, xb)
            if g + 1 < ng:
                load_x(g + 1)
            for _ in mx:
                pass
            checkpoint("mixer")
            if dbg and g == 0:
                dump(1, xb)
            ffn(xb, "f2", P_G2, do_norm=False, hoist_norm=((1 - xb, P_G1) if g + 1 < ng else None))
            store_x(g)
    except StopBuild:
        store_x(0)
    fw.final_wait("sp", s_xout + [s_dbg])
    assert stop is not None or wstate["next_use"] == npieces, (wstate, npieces)
    fw.replay()
    return nc, WTOT, fw


_CACHE = {}


def kernel(**inputs):
    x = np.asarray(inputs["x"], np.float32)
    positions = np.asarray(inputs["positions"], np.int32)
    if "nc" not in _CACHE:
        _CACHE["nc"] = build_program()
    nc, WTOT, fw = _CACHE["nc"]
    ws = pack_weights(inputs)
    assert ws.shape == (128, WTOT), (ws.shape, WTOT)
    par = pack_params(inputs)
    in_maps = []
    for c in range(NCORES):
        xc = np.ascontiguousarray(x[2 * c:2 * c + 2].reshape(TOK, D))
        pc = np.ascontiguousarray(positions[2 * c:2 * c + 2].reshape(NTILE, 128).T)
        in_maps.append({"x": xc, "pos": pc, "par": par, "ws": ws})
    res = run_bass_kernel_spmd(nc, in_maps, core_ids=list(range(NCORES)))
    out = np.stack([np.asarray(r["out"], np.float32).reshape(2, S, D) for r in res.results], axis=0)
    return out.reshape(16, S, D)
```

```python
import math
import numpy as np
import concourse.bass as bass
import concourse.mybir as mybir
from concourse.bass_utils import run_bass_kernel_spmd

F32 = mybir.dt.float32
BF16 = mybir.dt.bfloat16
I32 = mybir.dt.int32
AF = mybir.ActivationFunctionType
ALU = mybir.AluOpType
AX = mybir.AxisListType

NCORES = 8
S = 2048
D = 1024
DFF = 2816
NH = 8
DQK = 96
QL = 384
KVL = 256
TOK = 4096
G = 512
TPG = G // 128
NG = TOK // G
GPS = S // G
NTILE = TOK // 128
NFC = DFF // 128
EPS = 1e-6
NSLOT = 4
SLOTF = 5632
SM_SHIFT = math.sqrt(96.0)
KR = [(0, 4), (4, 11), (11, 22)]
NDELAY = 2

P_G1, P_GM, P_G2 = 0, 8, 16
P_GQ, P_GKV = 24, 27
P_GQH, P_GKH = 29, 125
P_GB = 221
P_CW = 237
P_INVF = 261
P_ID = 277
P_TRI = 405
P_GQC = 533
P_GKC = 534
NPAR = 535


class Dep:
    __slots__ = ("name", "w", "r", "excl")

    def __init__(self, name="", excl=False):
        self.name = name
        self.w = {}
        self.r = {}
        self.excl = excl


class Eng:
    def __init__(self, name):
        self.name = name
        self.sem = None
        self.count = 0
        self.ops = []
        self.known = {}


class FW:
    def __init__(self, nc):
        self.nc = nc
        self.sems = {}
        self.E = {k: Eng(k) for k in ("pe", "act", "dve", "pool", "sp")}
        for k, e in self.E.items():
            if k != "sp":
                e.sem = self.new_sem("e_" + k)
        self.dma_vals = {}
        self.nwaits = 0
        self.nops = 0

    def new_sem(self, name):
        self.sems[name] = self.nc.alloc_semaphore(name)
        return name

    def new_dma_sem(self, name):
        k = self.new_sem("d_" + name)
        self.dma_vals[k] = 0
        return k

    def _collect(self, eng, reads, writes):
        deps = {}

        def add(semk, val):
            if deps.get(semk, 0) < val:
                deps[semk] = val

        for d in reads:
            for semk, (val, en) in d.w.items():
                if en == eng.name and eng.name == "pe":
                    continue
                add(semk, val)
            if d.excl:
                for semk, (val, en) in d.r.items():
                    if en != eng.name:
                        add(semk, val)
        for d in writes:
            for semk, (val, en) in d.w.items():
                if en == eng.name and eng.name == "pe":
                    continue
                add(semk, val)
            for semk, (val, en) in d.r.items():
                if en == eng.name and eng.name == "pe":
                    continue
                add(semk, val)
        return deps

    def _emit_waits(self, eng, deps):
        for semk, val in deps.items():
            if eng.known.get(semk, 0) >= val:
                continue
            for e2 in self.E.values():
                if e2.sem == semk:
                    assert val <= e2.count, f"wait on unsignaled event {semk} {val}>{e2.count}"
            eng.known[semk] = val
            h = self.sems[semk]
            eng.ops.append(lambda e, h=h, val=val: e.wait_ge(h, val))
            self.nwaits += 1

    @staticmethod
    def _flat(lst):
        out = []
        for d in lst:
            if isinstance(d, (list, tuple)):
                out.extend(FW._flat(d))
            else:
                out.append(d)
        return out

    def op(self, engname, fn, reads=(), writes=(), signal=True):
        eng = self.E[engname]
        reads = self._flat(reads)
        writes = self._flat(writes)
        self._emit_waits(eng, self._collect(eng, reads, writes))
        self.nops += 1
        if signal:
            eng.count += 1
            val = eng.count
            h = self.sems[eng.sem]
            eng.ops.append(lambda e, fn=fn, h=h: fn(e).then_inc(h, 1))
        else:
            val = eng.count + 1
            eng.ops.append(lambda e, fn=fn: fn(e))
        ev = (val, eng.name)
        for d in writes:
            d.w = {eng.sem: ev}
            d.r = {}
        for d in reads:
            if any(d is w_ for w_ in writes):
                continue
            d.r[eng.sem] = ev
        return ev

    def dma(self, issuer, semk, fns, reads=(), writes=()):
        eng = self.E[issuer]
        reads = self._flat(reads)
        writes = self._flat(writes)
        self._emit_waits(eng, self._collect(eng, reads, writes))
        h = self.sems[semk]
        for fn in fns:
            self.dma_vals[semk] += 16
            eng.ops.append(lambda e, fn=fn, h=h: fn(e).then_inc(h, 16))
        ev = (self.dma_vals[semk], "dma:" + semk)
        for d in writes:
            d.w = {semk: ev}
            d.r = {}
        for d in reads:
            if any(d is w_ for w_ in writes):
                continue
            d.r[semk] = ev
        return ev

    def final_wait(self, engname, semks):
        eng = self.E[engname]
        for semk in semks:
            h = self.sems[semk]
            v = self.dma_vals[semk]
            if v > 0:
                eng.ops.append(lambda e, h=h, v=v: e.wait_ge(h, v))

    def replay(self):
        with self.nc.Block() as block:
            @block.tensor
            def _(e):
                for f in self.E["pe"].ops:
                    f(e)

            @block.scalar
            def _(e):
                for f in self.E["act"].ops:
                    f(e)

            @block.vector
            def _(e):
                for f in self.E["dve"].ops:
                    f(e)

            @block.gpsimd
            def _(e):
                for f in self.E["pool"].ops:
                    f(e)

            @block.sync
            def _(e):
                for f in self.E["sp"].ops:
                    f(e)


def weight_plan():
    pl = []
    for f in ("f1", "f2"):
        sub = []
        for c in range(NFC):
            sub.append((f"{f}gu{c}", 2 * 8 * 128))
        for hf in range(2):
            for kh, (k0, k1) in enumerate(KR):
                sub.append((f"{f}d{hf}{kh}", (k1 - k0) * 512))
        if f == "f1":
            pl += sub
            pl.append(("lat", 8 * 672))
            pl.append(("uqkv", 3 * 768 + 2 * 1024))
            for c in range(8):
                pl.append((f"conv{c}", 8 * 3 * 128))
            for c in range(8):
                pl.append((f"m4{c}", 28 * 128))
            for hf in range(2):
                pl.append((f"wo{hf}", 8 * 512))
        else:
            pl += sub
    return pl


def pack_weights(inp):
    f32 = np.float32

    def kt(w):
        K, N = w.shape
        return w.reshape(K // 128, 128, N).transpose(1, 0, 2)

    pieces = {}
    for f, pre in (("f1", "ffn1"), ("f2", "ffn2")):
        wg = kt(np.asarray(inp[pre + "_w_gate"], f32))
        wu = kt(np.asarray(inp[pre + "_w_up"], f32))
        wd = kt(np.asarray(inp[pre + "_w_down"], f32))
        for c in range(NFC):
            a = np.stack([wg[:, :, c * 128:(c + 1) * 128], wu[:, :, c * 128:(c + 1) * 128]], axis=1)
            pieces[f"{f}gu{c}"] = a.reshape(128, -1)
        for hf in range(2):
            for kh, (k0, k1) in enumerate(KR):
                pieces[f"{f}d{hf}{kh}"] = wd[:, k0:k1, hf * 512:(hf + 1) * 512].reshape(128, -1)
    w_in = kt(np.asarray(inp["w_in"], f32))
    pieces["lat"] = w_in[:, :, 0:672].reshape(128, -1)
    pieces["uq"] = kt(np.asarray(inp["w_uq"], f32)).reshape(128, -1)
    ukv = np.concatenate([np.asarray(inp["w_uk"], f32), np.asarray(inp["w_uv"], f32)], axis=1)
    pieces["ukv"] = kt(ukv).reshape(128, -1)
    pieces["uqkv"] = np.concatenate([pieces["uq"], pieces["ukv"]], axis=1)
    o_xc, o_gb, o_gc, o_gl = 672, 672 + 1024, 672 + 2048, 672 + 3072
    for c in range(8):
        a = np.stack([w_in[:, :, o + c * 128:o + (c + 1) * 128] for o in (o_xc, o_gb, o_gc)], axis=2)
        pieces[f"conv{c}"] = a.reshape(128, -1)
    wpc = kt(np.asarray(inp["w_proj_conv"], f32))
    wpa = kt(np.asarray(inp["w_proj_attn"], f32))
    for c in range(8):
        sl = slice(c * 128, (c + 1) * 128)
        a = np.concatenate([wpc[:, :, sl], wpa[:, :, sl],
                            w_in[:, :, o_gl + c * 128:o_gl + (c + 1) * 128],
                            w_in[:, :, o_gl + 1024 + c * 128:o_gl + 1024 + (c + 1) * 128]], axis=1)
        pieces[f"m4{c}"] = a.reshape(128, -1)
    wo = kt(np.asarray(inp["w_out"], f32))
    for hf in range(2):
        pieces[f"wo{hf}"] = wo[:, :, hf * 512:(hf + 1) * 512].reshape(128, -1)
    plan = weight_plan()
    arr = np.concatenate([pieces[n] for n, _ in plan], axis=1)
    for (n, F) in plan:
        assert pieces[n].shape[1] == F, (n, pieces[n].shape, F)
    return np.ascontiguousarray(arr, dtype=f32)


def pack_params(inp):
    f32 = np.float32
    par = np.zeros((128, NPAR), f32)

    def col(v, n):
        return np.asarray(v, f32).reshape(n, 128).T

    par[:, P_G1:P_G1 + 8] = col(inp["ffn1_norm"], 8)
    par[:, P_GM:P_GM + 8] = col(inp["mix_norm"], 8)
    par[:, P_G2:P_G2 + 8] = col(inp["ffn2_norm"], 8)
    par[:, P_GQ:P_GQ + 3] = col(inp["q_a_norm"], 3)
    par[:, P_GKV:P_GKV + 2] = col(inp["kv_a_norm"], 2)
    par[:, P_GQH:P_GQH + 96] = np.asarray(inp["q_head_norm"], f32)[None, :]
    par[:, P_GKH:P_GKH + 96] = np.asarray(inp["k_head_norm"], f32)[None, :]
    par[:, P_GB:P_GB + 16] = col(inp["gate_bias"], 16)
    cw = np.asarray(inp["conv_w"], f32)
    par[:, P_CW:P_CW + 24] = cw.reshape(3, 8, 128).transpose(2, 1, 0).reshape(128, 24)
    half = 16
    invf = (1.0 / (10000.0 ** (np.arange(half, dtype=np.float32) / np.float32(half)))).astype(f32)
    par[:, P_INVF:P_INVF + 16] = invf[None, :]
    par[:, P_ID:P_ID + 128] = np.eye(128, dtype=f32)
    par[:, P_TRI:P_TRI + 128] = np.triu(np.ones((128, 128), f32))
    par[:, P_GQC] = 1.0
    par[:, P_GKC] = 1.0
    par[0:64, P_GQC] = np.asarray(inp["q_head_norm"], f32)[0:64]
    par[0:64, P_GKC] = np.asarray(inp["k_head_norm"], f32)[0:64]
    return par


class StopBuild(Exception):
    pass


def build_program(ng=NG, dbg=False, stop=None):
    stage = {"n": 0}

    def checkpoint(name):
        if stop is not None and name == stop:
            raise StopBuild()

    nc = bass.Bass("TRN2", target_bir_lowering=False)
    plan = weight_plan()
    offs = []
    o = 0
    for n, F in plan:
        offs.append(o)
        o += F
    WTOT = o

    x_d = nc.dram_tensor("x", [TOK, D], F32, kind="ExternalInput").ap()
    pos_d = nc.dram_tensor("pos", [128, NTILE], I32, kind="ExternalInput").ap()
    par_d = nc.dram_tensor("par", [128, NPAR], F32, kind="ExternalInput").ap()
    ws_d = nc.dram_tensor("ws", [128, WTOT], F32, kind="ExternalInput").ap()
    out_d = nc.dram_tensor("out", [TOK, D], F32, kind="ExternalOutput").ap()
    if dbg:
        dbg_d = nc.dram_tensor("dbg", [4, G, D], F32, kind="ExternalOutput").ap()
        dbg_attn = nc.dram_tensor("dbg_attn", [128, TPG, 512], BF16, kind="ExternalOutput").ap()
        dbg_q = nc.dram_tensor("dbg_q", [96, NH, G], BF16, kind="ExternalOutput").ap()
        dbg_k = nc.dram_tensor("dbg_k", [96, NH, G], BF16, kind="ExternalOutput").ap()
        dbg_v = nc.dram_tensor("dbg_v", [128, TPG, NH, 65], BF16, kind="ExternalOutput").ap()

    fw = FW(nc)
    sb = nc.alloc_sbuf_tensor
    op = fw.op

    par = sb("par_sb", [128, NPAR], F32)
    d_par = Dep("par")
    pos_t = sb("pos_sb", [128, NTILE], I32)
    d_pos = Dep("pos")
    identb = sb("identb", [128, 128], BF16)
    d_id = Dep("id")
    trib = sb("trib", [128, 128], BF16)
    d_tri = Dep("tri")
    sinT = sb("sinT", [128, NTILE, 16], F32)
    cosT = sb("cosT", [128, NTILE, 16], F32)
    d_sin, d_cos = Dep("sin"), Dep("cos")
    negM = sb("negM", [128, 1], F32)
    d_negM = Dep("negM")
    epsT = sb("epsT", [128, 1], F32)
    d_epsT = Dep("epsT")

    xres = [sb(f"xres{i}", [128, TPG, D], F32) for i in range(2)]
    d_x = [[Dep(f"x{i}_{t}") for t in range(TPG)] for i in range(2)]
    hT = sb("hT", [128, 8, G], BF16)
    d_hT = [Dep(f"hT{t}") for t in range(TPG)]
    actT = sb("actT", [128, NFC, G], BF16)
    d_act = [Dep(f"act{c}") for c in range(NFC)]
    KT = sb("KT", [128, NH, S], BF16)
    d_KT = [Dep(f"KT{t}") for t in range(S // 128)]
    Vc = sb("Vc", [128, S // 128, NH, 65], BF16)
    d_V = [Dep(f"V{t}") for t in range(S // 128)]
    QT = sb("QT", [128, NH, G], BF16)
    d_QT = [Dep(f"QT{t}") for t in range(TPG)]
    attn_tok = sb("attn_tok", [128, TPG, 512], BF16)
    d_atok = [Dep(f"atok{t}") for t in range(TPG)]
    attnT = sb("attnT", [128, 4, G], BF16)
    d_attnT = [Dep(f"attnT{t}") for t in range(TPG)]
    convT = sb("convT", [128, 8, G], BF16)
    d_conv = [Dep(f"conv{c}") for c in range(8)]
    mrgT = QT
    d_mrg = [[Dep(f"mrg{c}")] + d_QT for c in range(8)]
    carry = sb("carry", [128, 8, 2], F32)
    d_carry = [Dep(f"carry{c}") for c in range(8)]

    wslot = [sb(f"wslot{i}", [128, SLOTF], BF16) for i in range(NSLOT)]
    d_w = [Dep(f"w{i}") for i in range(NSLOT)]
    s_w = [fw.new_dma_sem(f"w{i}") for i in range(NSLOT)]

    junk = sb("junk", [128, D], BF16)
    d_junk = Dep("junk")
    ss = sb("ss", [128, 16], F32)
    d_ss = Dep("ss")
    rr = sb("rr", [128, 16], F32)
    d_rr = Dep("rr")
    n_ra = sb("n_ra", [128, 16], F32)
    n_rb = sb("n_rb", [128, 16], F32)
    n_ri = sb("n_ri", [128, 16], I32)
    d_nra, d_nrb, d_nri = Dep("nra"), Dep("nrb"), Dep("nri")
    xn = [sb(f"xn{i}", [128, D], BF16) for i in range(2)]
    d_xn = [Dep(f"xn{i}") for i in range(2)]
    sg = [sb(f"sg{i}", [128, 512], F32) for i in range(2)]
    d_sg = [Dep(f"sg{i}") for i in range(2)]
    ssl2 = [sb(f"ssl{i}", [128, 4], F32) for i in range(2)]
    d_ssl2 = [Dep(f"ssl{i}") for i in range(2)]
    rl2 = [sb(f"rl{i}", [128, 4], F32) for i in range(2)]
    d_rl2 = [Dep(f"rl{i}") for i in range(2)]
    krf2 = [sb(f"krf{i}", [128, 32], F32) for i in range(2)]
    d_krf2 = [Dep(f"krf{i}") for i in range(2)]
    qlb = sb("qlb", [128, QL], BF16)
    kvb = sb("kvb", [128, KVL], BF16)
    krf = sb("krf", [128, 32], F32)
    d_qlb, d_kvb, d_krf = Dep("qlb"), Dep("kvb"), Dep("krf")
    lt5 = sb("lt5", [128, 5, 128], BF16)
    d_lt5 = Dep("lt5")
    def act_view(c0, nch, dtype, n):
        ap = actT[:, c0:c0 + nch, :].rearrange("p a b -> p (a b)")
        if dtype == F32:
            ap = ap.bitcast(F32)
        return ap[:, 0:n], [d_act[c] for c in range(c0, c0 + nch)]

    qs, d_qs = act_view(0, 3, F32, 768)
    ks, d_ks = act_view(3, 2, F32, 512)
    SQ_f, d_SQ = act_view(5, 6, F32, 16 * 96)
    qr_f, d_qr = act_view(11, 1, F32, 256)
    qf_f, d_qf = act_view(12, 2, BF16, 768)
    kf_f, d_kf = act_view(14, 2, BF16, 768)
    ssh = sb("ssh", [128, 16], F32)
    rh = sb("rh", [128, 16], F32)
    d_ssh, d_rh = Dep("ssh"), Dep("rh")
    d_sshq, d_sshk = Dep("sshq"), Dep("sshk")
    SQ = SQ_f.rearrange("p (h d) -> p h d", h=16)
    qr = qr_f.rearrange("p (h d) -> p h d", h=NH)
    qf = qf_f.rearrange("p (h d) -> p h d", h=NH)
    kf = kf_f.rearrange("p (h d) -> p h d", h=NH)
    rt = [sb(f"rt{i}", [128, NH, 16], F32) for i in range(4)]
    d_rt = [Dep(f"rt{i}") for i in range(4)]
    kb = sb("kb", [128, 32], F32)
    kbr = sb("kbr", [128, 32], F32)
    d_kb, d_kbr = Dep("kb"), Dep("kbr")
    kt4 = [sb(f"kt4{i}", [128, 16], F32) for i in range(4)]
    d_kt4 = [Dep(f"kt4{i}") for i in range(4)]
    pt, d_pt = [], []
    for i in range(4):
        a_, d_ = act_view(18 + i, 1, BF16, 512)
        pt.append(a_)
        d_pt.append(d_)
    rec = sb("rec", [128, 4], F32)
    d_rec = Dep("rec")
    units = [act_view(2 * u, 2, F32, 512) for u in range(8)]
    xcs = [units[0][0], units[1][0]]
    d_xcs = [units[0][1], units[1][1]]
    zb = [units[2][0], units[3][0]]
    d_zb = [units[2][1], units[3][1]]
    ub = [sb(f"ub{i}", [128, 514], F32) for i in range(2)]
    d_ub = [Dep(f"ub{i}") for i in range(2)]
    sa = [units[0][0], units[1][0]]
    d_sa = [units[0][1], units[1][1]]
    sbb = [units[2][0], units[3][0]]
    d_sbb = [units[2][1], units[3][1]]
    m1 = [units[4][0], units[5][0]]
    d_m1 = [units[4][1], units[5][1]]
    m2 = [units[6][0], units[7][0]]
    d_m2 = [units[6][1], units[7][1]]
    tA, d_tA = units[0]
    tB, d_tB = units[1]
    tC, d_tC = units[2]
    tI, d_tI = units[3]
    tA = tA.rearrange("p (t f) -> p t f", f=16)
    tB = tB.rearrange("p (t f) -> p t f", f=16)
    tC = tC.rearrange("p (t f) -> p t f", f=16)
    tI = tI.bitcast(I32).rearrange("p (t f) -> p t f", f=16)
    posf = sb("posf", [128, NTILE], F32)
    d_posf = Dep("posf")

    banks = [nc.alloc_psum_tensor(f"bank{i}", [128, 512], F32) for i in range(8)]
    d_bank = [Dep(f"bank{i}", excl=True) for i in range(8)]
    rot = {"i": 0, "n": 6}
    pending = [False] * 8
    bank_idx = {}

    def psum(hold=False):
        for _ in range(rot["n"]):
            i = rot["i"] % rot["n"]
            rot["i"] += 1
            if not pending[i]:
                break
        assert not pending[i], "no free PSUM bank"
        pending[i] = hold
        return banks[i], d_bank[i]

    def prelease(dep):
        pending[d_bank.index(dep)] = False

    def bview(bank):
        return bank[:].bitcast(BF16).rearrange("p (a b) -> p a b", a=8)

    s_set = fw.new_dma_sem("set")
    s_xin = [fw.new_dma_sem(f"xin{i}") for i in range(2)]
    s_xout = [fw.new_dma_sem(f"xout{i}") for i in range(2)]

    npieces = len(plan) * ng
    wstate = {"next_issue": 0, "next_use": 0, "free": list(range(NSLOT)), "slot_of": {}, "held": {}}

    def w_fill():
        while wstate["free"] and wstate["next_issue"] < npieces:
            i = wstate["next_issue"]
            wstate["next_issue"] += 1
            n, F = plan[i % len(plan)]
            off = offs[i % len(plan)]
            s_ = wstate["free"].pop(0)
            wstate["slot_of"][i] = s_
            fw.dma("pool", s_w[s_],
                   [lambda e, s_=s_, off=off, F=F: e.dma_start(out=wslot[s_][:, 0:F], in_=ws_d[:, off:off + F])],
                   writes=[d_w[s_]])

    def w_get(name, keep=()):
        i = wstate["next_use"]
        wstate["next_use"] += 1
        n, F = plan[i % len(plan)]
        assert n == name, (n, name)
        for hn in list(wstate["held"].keys()):
            if hn not in keep:
                hi = wstate["held"].pop(hn)
                wstate["free"].append(wstate["slot_of"].pop(hi))
        w_fill()
        assert i in wstate["slot_of"], ("weight piece not issued (no free slot)", name)
        wstate["held"][name] = i
        s_ = wstate["slot_of"][i]
        return wslot[s_], d_w[s_], F

    def rsqrt_newton(dst, d_dst, src, d_src, n, mul, eps=EPS):
        ra = n_ra[:, 0:n]
        op("act", lambda e: e.activation(out=ra, in_=src, func=AF.Ln, scale=mul, bias=epsT[:]),
           reads=[d_src, d_epsT], writes=[d_nra])
        op("act", lambda e: e.activation(out=dst, in_=ra, func=AF.Exp, scale=-0.5), reads=[d_nra], writes=[d_dst])

    d_ssc = [Dep(f"ssc{i}") for i in range(TPG)]
    d_rac = [Dep(f"rac{i}") for i in range(TPG)]
    d_rrc = [Dep(f"rrc{i}") for i in range(TPG)]

    def norm_begin():
        op("dve", lambda e: e.memset(ss[:, 0:TPG], 0.0), writes=d_ssc)

    def norm_stats(xb, tt):
        c1 = slice(tt, tt + 1)
        op("act", lambda e: e.activation(out=junk[:], in_=xres[xb][:, tt, :], func=AF.Square, accum_out=ss[:, c1]),
           reads=[d_x[xb][tt]], writes=[d_junk, d_ssc[tt]])
        op("act", lambda e: e.activation(out=n_rb[:, c1], in_=ss[:, c1], func=AF.Ln, scale=1.0 / D, bias=epsT[:]),
           reads=[d_ssc[tt], d_epsT], writes=[d_rac[tt]])
        op("act", lambda e: e.activation(out=rr[:, c1], in_=n_rb[:, c1], func=AF.Exp, scale=-0.5),
           reads=[d_rac[tt]], writes=[d_rrc[tt]])

    def norm_apply(xb, gcol, tt):
        b = tt % 2
        c1 = slice(tt, tt + 1)
        op("act", lambda e: e.activation(out=xn[b][:], in_=xres[xb][:, tt, :], func=AF.Copy, scale=rr[:, c1]),
           reads=[d_x[xb][tt], d_rrc[tt]], writes=[d_xn[b]])
        tb_, d_tb = banks[6 + b], d_bank[6 + b]
        tv = bview(tb_)
        for k in range(8):
            op("pe", lambda e, k=k: e.transpose(out=tv[:, k, :], in_=xn[b][:, k * 128:(k + 1) * 128], identity=identb[:]),
               reads=[d_xn[b], d_id], writes=[d_tb], signal=(k == 7))
        op("dve", lambda e: e.tensor_tensor(
            out=hT[:, :, tt * 128:(tt + 1) * 128], in0=tv,
            in1=par[:, gcol:gcol + 8, None].broadcast_to([128, 8, 128]), op=ALU.mult),
           reads=[d_tb, d_par], writes=[d_hT[tt]])

    def norm_tail_step(xb, gcol, tt, tail_hook=None):
        if tt == TPG - 1:
            for t2 in range(max(0, tt - NDELAY), tt):
                norm_apply(xb, gcol, t2)
            if tail_hook is not None:
                tail_hook()
            norm_stats(xb, tt)
            norm_apply(xb, gcol, tt)
        else:
            if tt >= NDELAY:
                norm_apply(xb, gcol, tt - NDELAY)
            norm_stats(xb, tt)

    def norm_to_hT(xb, gcol):
        norm_begin()
        for tt in range(TPG):
            norm_stats(xb, tt)
            norm_apply(xb, gcol, tt)

    def ffn(xb, f, gcol, do_norm, same_norm=None, hoist_norm=None, tail_hook=None):
        if do_norm:
            norm_to_hT(xb, gcol)
        checkpoint("norm")
        for c in range(NFC):
            wt, dw, F = w_get(f"{f}gu{c}")
            wv = wt[:, 0:F].rearrange("p (a k m) -> p a k m", a=2, k=8)
            bg, dbg = psum()
            bu, dbu = psum()
            for k in range(8):
                op("pe", lambda e, k=k, wv=wv, bg=bg: e.matmul(bg[:], lhsT=wv[:, 0, k, :], rhs=hT[:, k, :],
                                                               start=(k == 0), stop=(k == 7)),
                   reads=[dw] + d_hT, writes=[dbg], signal=(k == 7))
            for k in range(8):
                op("pe", lambda e, k=k, wv=wv, bu=bu: e.matmul(bu[:], lhsT=wv[:, 1, k, :], rhs=hT[:, k, :],
                                                               start=(k == 0), stop=(k == 7)),
                   reads=[dw] + d_hT, writes=[dbu], signal=(k == 7))
            b = c % 2
            op("act", lambda e, b=b, bg=bg: e.activation(out=sg[b][:], in_=bg[:], func=AF.Silu),
               reads=[dbg], writes=[d_sg[b]])
            op("dve", lambda e, b=b, bu=bu, c=c: e.tensor_tensor(out=actT[:, c, :], in0=sg[b][:], in1=bu[:], op=ALU.mult),
               reads=[d_sg[b], dbu], writes=[d_act[c]])
        checkpoint("ffnup")
        if hoist_norm is not None:
            norm_begin()
            for tt in range(TPG):
                norm_stats(hoist_norm[0], tt)
        if same_norm is not None:
            norm_begin()

        def evac(tt, hf, ab, dab):
            op("dve", lambda e: e.scalar_tensor_tensor(
                out=xres[xb][:, tt, hf * 512:(hf + 1) * 512], in0=ab[:], scalar=0.5,
                in1=xres[xb][:, tt, hf * 512:(hf + 1) * 512], op0=ALU.mult, op1=ALU.add),
               reads=[dab, d_x[xb][tt]], writes=[d_x[xb][tt]])

        bi = 0
        for hf in range(2):
            accs = [psum() for _ in range(TPG)]
            for kh, (k0, k1) in enumerate(KR):
                wt, dw, F = w_get(f"{f}d{hf}{kh}")
                wv = wt[:, 0:F].rearrange("p (k n) -> p k n", k=k1 - k0)
                final = (hf == 1 and kh == len(KR) - 1)
                for tt in range(TPG):
                    ab, dab = accs[tt]
                    for kc in range(k0, k1):
                        op("pe", lambda e, kc=kc, k0=k0, tt=tt, wv=wv, ab=ab: e.matmul(
                            ab[:], lhsT=actT[:, kc, tt * 128:(tt + 1) * 128], rhs=wv[:, kc - k0, :],
                            start=(kc == 0), stop=(kc == NFC - 1)),
                           reads=[dw, d_act[kc]], writes=[dab], signal=(kc == k1 - 1))
                    if final:
                        evac(tt, hf, ab, dab)
                        if same_norm is not None:
                            norm_tail_step(xb, same_norm, tt, tail_hook)
                if hoist_norm is not None and bi in (1, 2, 4, 5):
                    norm_apply(hoist_norm[0], hoist_norm[1], {1: 0, 2: 1, 4: 2, 5: 3}[bi])
                bi += 1
            if hf == 0:
                for tt in range(TPG):
                    evac(tt, hf, *accs[tt])
    def rope(dst1, dst2, t1, t2, cosb, sinb, tmp, d_tmp, reads, writes, eng="dve"):
        a, b_, c_, d_ = tmp
        da, db, dc, dd = d_tmp
        op(eng, lambda e: e.tensor_tensor(out=a, in0=t1, in1=cosb, op=ALU.mult), reads=reads, writes=[da])
        op(eng, lambda e: e.tensor_tensor(out=b_, in0=t2, in1=sinb, op=ALU.mult), reads=reads, writes=[db])
        op(eng, lambda e: e.tensor_tensor(out=c_, in0=t1, in1=sinb, op=ALU.mult), reads=reads, writes=[dc])
        op(eng, lambda e: e.tensor_tensor(out=d_, in0=t2, in1=cosb, op=ALU.mult), reads=reads, writes=[dd])
        op(eng, lambda e: e.tensor_tensor(out=dst1, in0=a, in1=b_, op=ALU.subtract), reads=[da, db], writes=writes)
        op(eng, lambda e: e.tensor_tensor(out=dst2, in0=c_, in1=d_, op=ALU.add), reads=[dc, dd], writes=writes)

    def mixer(xb, g, same_norm=None):
        gi = g % GPS
        tile0 = g * TPG
        wl, dwl, F = w_get("lat", keep=("f1d12",))
        wlv = wl[:, 0:F].rearrange("p (k n) -> p k n", k=8)
        wqk, dwq, F = w_get("uqkv", keep=("lat",))
        dwk = dwq
        wqv = wqk[:, 0:2304].rearrange("p (k n) -> p k n", k=3)
        wkv = wqk[:, 2304:4352].rearrange("p (k n) -> p k n", k=2)
        T_ = {}

        def early1(tt):
            b = tt % 2
            tsl = slice(tt * 128, (tt + 1) * 128)
            pA, dpA = psum()
            pB, dpB = psum()
            for k in range(8):
                op("pe", lambda e, k=k: e.matmul(pA[:, 0:384], lhsT=hT[:, k, tsl], rhs=wlv[:, k, 0:384],
                                                 start=(k == 0), stop=(k == 7)),
                   reads=[dwl, d_hT[tt]], writes=[dpA], signal=(k == 7))
            for k in range(8):
                op("pe", lambda e, k=k: e.matmul(pB[:, 0:288], lhsT=hT[:, k, tsl], rhs=wlv[:, k, 384:672],
                                                 start=(k == 0), stop=(k == 7)),
                   reads=[dwl, d_hT[tt]], writes=[dpB], signal=(k == 7))
            op("dve", lambda e: e.memset(ssl2[b][:], 0.0), writes=[d_ssl2[b]])
            op("act", lambda e: e.activation(out=junk[:, 0:384], in_=pA[:, 0:384], func=AF.Square, scale=math.sqrt(float(KVL) / QL),
                                              accum_out=ssl2[b][:, 0:1]),
               reads=[dpA], writes=[d_junk, d_ssl2[b]])
            op("act", lambda e: e.activation(out=qlb[:], in_=pA[:, 0:384], func=AF.Copy), reads=[dpA], writes=[d_qlb])
            op("act", lambda e: e.activation(out=junk[:, 0:256], in_=pB[:, 0:256], func=AF.Square, accum_out=ssl2[b][:, 1:2]),
               reads=[dpB], writes=[d_junk, d_ssl2[b]])
            op("dve", lambda e: e.tensor_copy(out=kvb[:], in_=pB[:, 0:256]), reads=[dpB], writes=[d_kvb])
            op("dve", lambda e: e.tensor_copy(out=krf2[b][:], in_=pB[:, 256:288]), reads=[dpB], writes=[d_krf2[b]])
            rsqrt_newton(rl2[b][:, 0:2], d_rl2[b], ssl2[b][:, 0:2], d_ssl2[b], 2, 1.0 / KVL)
            op("act", lambda e: e.activation(out=SQ[:, 8:16, 64:96], in_=krf2[b][:, None, :].broadcast_to([128, NH, 32]),
                                              func=AF.Square),
               reads=[d_krf2[b]], writes=[d_SQ])

        def early2(tt):
            tb_, d_tb = psum()
            tv = bview(tb_)
            for j in range(3):
                op("pe", lambda e, j=j: e.transpose(out=tv[:, j, :], in_=qlb[:, j * 128:(j + 1) * 128], identity=identb[:]),
                   reads=[d_qlb, d_id], writes=[d_tb], signal=False)
            for j in range(2):
                op("pe", lambda e, j=j: e.transpose(out=tv[:, 3 + j, :], in_=kvb[:, j * 128:(j + 1) * 128], identity=identb[:]),
                   reads=[d_kvb, d_id], writes=[d_tb], signal=(j == 1))
            op("dve", lambda e: e.tensor_tensor(out=lt5[:], in0=tv[:, 0:5, :],
                                                 in1=par[:, P_GQ:P_GQ + 5, None].broadcast_to([128, 5, 128]), op=ALU.mult),
               reads=[d_tb, d_par], writes=[d_lt5])
            pq0, dpq0 = psum(hold=True)
            pq1, dpq1 = psum(hold=True)
            pk, dpk = psum(hold=True)
            pv, dpv = psum(hold=True)
            for j in range(3):
                op("pe", lambda e, j=j: e.matmul(pq0[:, 0:384], lhsT=lt5[:, j, :], rhs=wqv[:, j, 0:384],
                                                 start=(j == 0), stop=(j == 2)),
                   reads=[dwq, d_lt5], writes=[dpq0], signal=(j == 2))
            for j in range(3):
                op("pe", lambda e, j=j: e.matmul(pq1[:, 0:384], lhsT=lt5[:, j, :], rhs=wqv[:, j, 384:768],
                                                 start=(j == 0), stop=(j == 2)),
                   reads=[dwq, d_lt5], writes=[dpq1], signal=(j == 2))
            for j in range(2):
                op("pe", lambda e, j=j: e.matmul(pk[:], lhsT=lt5[:, 3 + j, :], rhs=wkv[:, j, 0:512],
                                                 start=(j == 0), stop=(j == 1)),
                   reads=[dwk, d_lt5], writes=[dpk], signal=(j == 1))
            for j in range(2):
                op("pe", lambda e, j=j: e.matmul(pv[:], lhsT=lt5[:, 3 + j, :], rhs=wkv[:, j, 512:1024],
                                                 start=(j == 0), stop=(j == 1)),
                   reads=[dwk, d_lt5], writes=[dpv], signal=(j == 1))
            T_[tt] = (pq0, dpq0, pq1, dpq1, pk, dpk, pv, dpv)

        def late6(tt):
            b = tt % 2
            st = gi * TPG + tt
            pq0, dpq0, pq1, dpq1, pk, dpk, pv, dpv = T_.pop(tt)
            for d_ in (dpq0, dpq1, dpk, dpv):
                prelease(d_)
            rl_, d_rl_ = rl2[b], d_rl2[b]
            op("act", lambda e: e.activation(out=SQ[:, 0:4, :], in_=pq0[:, 0:384].rearrange("p (h d) -> p h d", h=4),
                                              func=AF.Square, scale=rl_[:, 0:1]),
               reads=[dpq0, d_rl_], writes=[d_SQ])
            op("act", lambda e: e.activation(out=SQ[:, 4:8, :], in_=pq1[:, 0:384].rearrange("p (h d) -> p h d", h=4),
                                              func=AF.Square, scale=rl_[:, 0:1]),
               reads=[dpq1, d_rl_], writes=[d_SQ])
            op("act", lambda e: e.activation(out=SQ[:, 8:16, 0:64], in_=pk[:].rearrange("p (h d) -> p h d", h=NH),
                                              func=AF.Square, scale=rl_[:, 1:2]),
               reads=[dpk, d_rl_], writes=[d_SQ])
            op("dve", lambda e: e.tensor_reduce(out=ssh[:, 0:16], in_=SQ[:], axis=AX.X, op=ALU.add), reads=[d_SQ], writes=[d_sshq, d_sshk])
            op("act", lambda e: e.activation(out=qs[:, 0:384], in_=pq0[:, 0:384], func=AF.Copy, scale=rl_[:, 0:1]),
               reads=[dpq0, d_rl_], writes=[d_qs])
            op("act", lambda e: e.activation(out=qs[:, 384:768], in_=pq1[:, 0:384], func=AF.Copy, scale=rl_[:, 0:1]),
               reads=[dpq1, d_rl_], writes=[d_qs])
            op("act", lambda e: e.activation(out=ks[:], in_=pk[:], func=AF.Copy, scale=rl_[:, 1:2]),
               reads=[dpk, d_rl_], writes=[d_ks])
            op("act", lambda e: e.activation(out=Vc[:, st, :, 0:64], in_=pv[:].rearrange("p (h d) -> p h d", h=NH),
                                              func=AF.Copy, scale=rl_[:, 1:2]),
               reads=[dpv, d_rl_], writes=[d_V[st]])

        def late78(tt):
            b = tt % 2
            rsqrt_newton(rh[:, 0:16], d_rh, ssh[:, 0:16], [d_sshq, d_sshk], 16, 1.0 / DQK)

        def late9(tt):
            b = tt % 2
            gt_ = tile0 + tt
            qsv = qs[:].rearrange("p (h d) -> p h d", h=NH)
            ksv = ks[:].rearrange("p (h d) -> p h d", h=NH)
            op("dve", lambda e: e.tensor_tensor(out=qr[:], in0=qsv[:, :, 64:96], in1=rh[:, 0:8, None].broadcast_to([128, NH, 32]),
                                                 op=ALU.mult), reads=[d_qs, d_rh], writes=[d_qr])
            op("dve", lambda e: e.tensor_tensor(out=qr[:], in0=qr[:],
                                                 in1=par[:, None, P_GQH + 64:P_GQH + 96].broadcast_to([128, NH, 32]), op=ALU.mult),
               reads=[d_qr, d_par], writes=[d_qr])
            cosb = cosT[:, gt_, None, :].broadcast_to([128, NH, 16])
            sinb = sinT[:, gt_, None, :].broadcast_to([128, NH, 16])
            rope(qf[:, :, 64:80], qf[:, :, 80:96], qr[:, :, 0:16], qr[:, :, 16:32], cosb, sinb,
                 [t[:] for t in rt], d_rt, [d_qr, d_cos, d_sin], [d_qf])
            op("dve", lambda e: e.tensor_tensor(out=qf[:, :, 0:64], in0=qsv[:, :, 0:64], in1=rh[:, 0:8, None].broadcast_to([128, NH, 64]),
                                                 op=ALU.mult), reads=[d_qs, d_rh], writes=[d_qf])
            op("dve", lambda e: e.tensor_tensor(out=kf[:, :, 0:64], in0=ksv, in1=rh[:, 8:16, None].broadcast_to([128, NH, 64]),
                                                 op=ALU.mult), reads=[d_ks, d_rh], writes=[d_kf])
            op("dve", lambda e: e.tensor_tensor(out=kb[:], in0=krf2[b][:], in1=par[:, P_GKH + 64:P_GKH + 96], op=ALU.mult),
               reads=[d_krf2[b], d_par], writes=[d_kb])
            rope(kbr[:, 0:16], kbr[:, 16:32], kb[:, 0:16], kb[:, 16:32], cosT[:, gt_, :], sinT[:, gt_, :],
                 [t[:] for t in kt4], d_kt4, [d_kb, d_cos, d_sin], [d_kbr])
            op("dve", lambda e: e.tensor_tensor(out=kf[:, :, 64:96], in0=kbr[:, None, :].broadcast_to([128, NH, 32]),
                                                 in1=rh[:, 8:16, None].broadcast_to([128, NH, 32]), op=ALU.mult),
               reads=[d_kbr, d_rh], writes=[d_kf])

        def late10(tt):
            st = gi * TPG + tt
            tsl = slice(tt * 128, (tt + 1) * 128)
            tq, d_tq = psum()
            tk, d_tk = psum()
            tqv, tkv = bview(tq), bview(tk)
            for h in range(NH):
                op("pe", lambda e, h=h: e.transpose(out=tqv[0:96, h, :], in_=qf[:, h, :], identity=identb[:]),
                   reads=[d_qf, d_id], writes=[d_tq], signal=(h == NH - 1))
            for h in range(NH):
                op("pe", lambda e, h=h: e.transpose(out=tkv[0:96, h, :], in_=kf[:, h, :], identity=identb[:]),
                   reads=[d_kf, d_id], writes=[d_tk], signal=(h == NH - 1))
            op("dve", lambda e: e.tensor_scalar(out=QT[0:96, :, tsl], in0=tqv[0:96, :, :],
                                                 scalar1=par[0:96, P_GQC:P_GQC + 1], scalar2=None, op0=ALU.mult),
               reads=[d_tq, d_par], writes=[d_QT[tt]])
            op("dve", lambda e: e.tensor_scalar(out=KT[0:96, :, st * 128:(st + 1) * 128], in0=tkv[0:96, :, :],
                                                 scalar1=par[0:96, P_GKC:P_GKC + 1], scalar2=None, op0=ALU.mult),
               reads=[d_tk, d_par], writes=[d_KT[st]])


        CV = {}

        def conv_pe(c, keep):
            wt, dw, F = w_get(f"conv{c}", keep=keep)
            wv = wt[:, 0:F].rearrange("p (k i m) -> p k i m", k=8, i=3)
            bks = [psum(hold=True), psum(hold=True), psum(hold=True)]
            for i_, (pb_, dpb_) in enumerate(bks):
                for k in range(8):
                    op("pe", lambda e, k=k, i_=i_, pb_=pb_, wv=wv: e.matmul(pb_[:], lhsT=wv[:, k, i_, :], rhs=hT[:, k, :],
                                                                           start=(k == 0), stop=(k == 7)),
                       reads=[dw] + d_hT, writes=[dpb_], signal=(k == 7))
            CV[c] = bks

        def conv_rest(c):
            (pxc, dpxc), (pgb, dpgb), (pgc, dpgc) = CV.pop(c)
            for d_ in (dpxc, dpgb, dpgc):
                prelease(d_)
            b = c % 2
            op("act", lambda e: e.activation(out=xcs[b][:], in_=pxc[:], func=AF.Copy), reads=[dpxc], writes=[d_xcs[b]])
            if gi == 0:
                op("dve", lambda e: e.memset(ub[b][:, 0:2], 0.0), writes=[d_ub[b]])
            else:
                op("dve", lambda e: e.tensor_copy(out=ub[b][:, 0:2], in_=carry[:, c, :]), reads=[d_carry[c]], writes=[d_ub[b]])
            op("dve", lambda e: e.tensor_tensor(out=ub[b][:, 2:514], in0=pgc[:], in1=xcs[b][:], op=ALU.mult),
               reads=[dpgc, d_xcs[b]], writes=[d_ub[b]])
            op("dve", lambda e: e.tensor_copy(out=carry[:, c, :], in_=ub[b][:, 512:514]), reads=[d_ub[b]], writes=[d_carry[c]])
            cw0 = P_CW + c * 3
            op("dve", lambda e: e.tensor_scalar(out=zb[b][:], in0=ub[b][:, 0:512], scalar1=par[:, cw0:cw0 + 1],
                                                 scalar2=None, op0=ALU.mult),
               reads=[d_ub[b], d_par], writes=[d_zb[b]])
            for k_ in (1, 2):
                op("dve", lambda e, k_=k_: e.scalar_tensor_tensor(
                    out=zb[b][:], in0=ub[b][:, k_:k_ + 512], scalar=par[:, cw0 + k_:cw0 + k_ + 1], in1=zb[b][:],
                    op0=ALU.mult, op1=ALU.add),
                   reads=[d_ub[b], d_par, d_zb[b]], writes=[d_zb[b]])
            op("dve", lambda e: e.tensor_tensor(out=convT[:, c, :], in0=pgb[:], in1=zb[b][:], op=ALU.mult),
               reads=[dpgb, d_zb[b]], writes=[d_conv[c]])

        rot["n"] = 8
        rot["i"] = 0
        early1(0)
        yield
        early2(0)
        for tt in range(TPG):
            last = (tt == TPG - 1)
            late6(tt)
            late78(tt)
            if not last:
                early1(tt + 1)
            else:
                conv_pe(0, ())
            if not last:
                early2(tt + 1)
            else:
                conv_pe(1, ())
            late9(tt)
            late10(tt)
        conv_rest(0)
        conv_rest(1)
        rot["n"] = 6
        for c in range(2, 8):
            conv_pe(c, ())
            conv_rest(c)
        checkpoint("m1")

        qb = gi
        nkt = 4 * qb + 4
        sc = 1.0 / math.sqrt(DQK)
        steps = [(h, kt) for h in range(NH) for kt in range(nkt)]
        st_state = {}

        def emit_S(si):
            h, kt = steps[si]
            jmin = max(0, kt - 4 * qb)
            c0 = jmin * 128
            n = 512 - c0
            ps, dps = psum()
            op("pe", lambda e, ps=ps, h=h, kt=kt, c0=c0, n=n: e.matmul(
                ps[:, 0:n], lhsT=KT[0:96, h, kt * 128:(kt + 1) * 128], rhs=QT[0:96, h, c0:512], start=True, stop=True),
               reads=[d_KT[kt]] + d_QT[jmin:], writes=[dps])
            pb = si % 4
            op("act", lambda e, ps=ps, pb=pb, n=n: e.activation(out=pt[pb][:, 0:n], in_=ps[:, 0:n], func=AF.Exp,
                                                                 scale=sc, bias=negM[:]),
               reads=[dps, d_negM], writes=[d_pt[pb]])
            if kt >= 4 * qb:
                op("dve", lambda e, pb=pb: e.tensor_tensor(out=pt[pb][:, 0:128], in0=pt[pb][:, 0:128], in1=trib[:], op=ALU.mult),
                   reads=[d_pt[pb], d_tri], writes=[d_pt[pb]])
            st_state[si] = (pb, jmin)

        LOOK = 3
        nxt = 0
        for si, (h, kt) in enumerate(steps):
            while nxt < len(steps) and nxt <= si + LOOK:
                emit_S(nxt)
                nxt += 1
            pb, jmin = st_state.pop(si)
            Ob, dOb = banks[6 + (h % 2)], d_bank[6 + (h % 2)]
            Ov = Ob[:, 0:260].rearrange("p (j d) -> p j d", j=4)
            for j in range(jmin, 4):
                last = (kt == 4 * qb + j)
                op("pe", lambda e, j=j, jmin=jmin, pb=pb, kt=kt, h=h, Ov=Ov, last=last: e.matmul(
                    Ov[:, j, :], lhsT=pt[pb][:, (j - jmin) * 128:(j - jmin + 1) * 128], rhs=Vc[:, kt, h, :],
                    start=(kt == 0 and j == 0), stop=last, skip_group_check=True),
                   reads=[d_pt[pb], d_V[kt]], writes=[dOb], signal=(j == 3))
            if kt == nkt - 1:
                op("dve", lambda e, Ov=Ov: e.reciprocal(out=rec[:], in_=Ov[:, :, 64]), reads=[dOb], writes=[d_rec])
                op("dve", lambda e, Ov=Ov, h=h: e.tensor_tensor(out=attn_tok[:, :, h * 64:(h + 1) * 64], in0=Ov[:, :, 0:64],
                                                                 in1=rec[:, :, None].broadcast_to([128, 4, 64]), op=ALU.mult),
                   reads=[dOb, d_rec], writes=d_atok)
        for j in range(TPG):
            tb_, d_tb = banks[6 + (j % 2)], d_bank[6 + (j % 2)]
            tv = bview(tb_)
            for fc in range(4):
                op("pe", lambda e, j=j, fc=fc, tv=tv: e.transpose(out=tv[:, fc, :], in_=attn_tok[:, j, fc * 128:(fc + 1) * 128],
                                                                 identity=identb[:]),
                   reads=[d_atok[j], d_id], writes=[d_tb], signal=(fc == 3))
            op("act", lambda e, j=j, tv=tv: e.activation(out=attnT[:, :, j * 128:(j + 1) * 128], in_=tv[:, 0:4, :], func=AF.Copy),
               reads=[d_tb], writes=[d_attnT[j]])

        checkpoint("m2")
        if dbg and g == 0:
            fw.dma("sp", s_dbg, [lambda e: e.dma_start(out=dbg_attn[:, :, :], in_=attn_tok[:]),
                                 lambda e: e.dma_start(out=dbg_q[:, :, :], in_=QT[0:96, :, :]),
                                 lambda e: e.dma_start(out=dbg_k[:, :, :], in_=KT[0:96, :, 0:G]),
                                 lambda e: e.dma_start(out=dbg_v[:, :, :, :], in_=Vc[:, 0:TPG, :, :])],
                   reads=[d_atok, d_QT, d_KT[0:TPG], d_V[0:TPG]])
        checkpoint("m3")
        for c in range(8):
            wt, dw, F = w_get(f"m4{c}")
            wv = wt[:, 0:F].rearrange("p (k m) -> p k m", k=28)
            pyb, dpyb = psum()
            pya, dpya = psum()
            pga, dpga = psum()
            pgb, dpgb = psum()
            for k in range(8):
                op("pe", lambda e, k=k, wv=wv, pyb=pyb: e.matmul(pyb[:], lhsT=wv[:, k, :], rhs=convT[:, k, :],
                                                                 start=(k == 0), stop=(k == 7)),
                   reads=[dw, d_conv[k]], writes=[dpyb], signal=(k == 7))
            for k in range(4):
                op("pe", lambda e, k=k, wv=wv, pya=pya: e.matmul(pya[:], lhsT=wv[:, 8 + k, :], rhs=attnT[:, k, :],
                                                                 start=(k == 0), stop=(k == 3)),
                   reads=[dw] + d_attnT, writes=[dpya], signal=(k == 3))
            for k in range(8):
                op("pe", lambda e, k=k, wv=wv, pga=pga: e.matmul(pga[:], lhsT=wv[:, 12 + k, :], rhs=hT[:, k, :],
                                                                 start=(k == 0), stop=(k == 7)),
                   reads=[dw] + d_hT, writes=[dpga], signal=(k == 7))
            for k in range(8):
                op("pe", lambda e, k=k, wv=wv, pgb=pgb: e.matmul(pgb[:], lhsT=wv[:, 20 + k, :], rhs=hT[:, k, :],
                                                                 start=(k == 0), stop=(k == 7)),
                   reads=[dw] + d_hT, writes=[dpgb], signal=(k == 7))
            b = c % 2
            op("act", lambda e, b=b, c=c, pga=pga: e.activation(out=sa[b][:], in_=pga[:], func=AF.Sigmoid,
                                                                 bias=par[:, P_GB + c:P_GB + c + 1]),
               reads=[dpga, d_par], writes=[d_sa[b]])
            op("act", lambda e, b=b, c=c, pgb=pgb: e.activation(out=sbb[b][:], in_=pgb[:], func=AF.Sigmoid,
                                                                 bias=par[:, P_GB + 8 + c:P_GB + 9 + c]),
               reads=[dpgb, d_par], writes=[d_sbb[b]])
            op("dve", lambda e, b=b, pya=pya: e.tensor_tensor(out=m1[b][:], in0=pya[:], in1=sa[b][:], op=ALU.mult),
               reads=[dpya, d_sa[b]], writes=[d_m1[b]])
            op("dve", lambda e, b=b, pyb=pyb: e.tensor_tensor(out=m2[b][:], in0=pyb[:], in1=sbb[b][:], op=ALU.mult),
               reads=[dpyb, d_sbb[b]], writes=[d_m2[b]])
            op("dve", lambda e, b=b, c=c: e.tensor_tensor(out=mrgT[:, c, :], in0=m1[b][:], in1=m2[b][:], op=ALU.add),
               reads=[d_m1[b], d_m2[b]], writes=[d_mrg[c]])

        checkpoint("m4")
        wo_ = []
        for hf in range(2):
            wt, dw, F = w_get(f"wo{hf}", keep=("wo0",))
            wo_.append((wt[:, 0:F].rearrange("p (k n) -> p k n", k=8), dw))
        if same_norm is not None:
            norm_begin()
        for tt in range(TPG):
            for hf in range(2):
                wv, dw = wo_[hf]
                po, dpo = psum()
                for k in range(8):
                    op("pe", lambda e, k=k, tt=tt, wv=wv, po=po: e.matmul(po[:], lhsT=mrgT[:, k, tt * 128:(tt + 1) * 128],
                                                                         rhs=wv[:, k, :], start=(k == 0), stop=(k == 7)),
                       reads=[dw, d_mrg[k]], writes=[dpo], signal=(k == 7))
                op("dve", lambda e, tt=tt, hf=hf, po=po: e.tensor_tensor(
                    out=xres[xb][:, tt, hf * 512:(hf + 1) * 512], in0=po[:], in1=xres[xb][:, tt, hf * 512:(hf + 1) * 512],
                    op=ALU.add),
                   reads=[dpo, d_x[xb][tt]], writes=[d_x[xb][tt]])
            if same_norm is not None:
                norm_tail_step(xb, same_norm, tt)

    def load_x(g):
        xb = g % 2
        src = x_d[g * G:(g + 1) * G, :].rearrange("(t p) d -> p t d", p=128)
        fw.dma("sp", s_xin[xb], [lambda e, xb=xb, src=src: e.dma_start(out=xres[xb][:], in_=src)], writes=d_x[xb])

    def store_x(g):
        xb = g % 2
        dst = out_d[g * G:(g + 1) * G, :].rearrange("(t p) d -> p t d", p=128)
        fw.dma("sp", s_xout[xb], [lambda e, xb=xb, dst=dst: e.dma_start(out=dst, in_=xres[xb][:])], reads=d_x[xb])

    fw.dma("sp", s_set, [lambda e: e.dma_start(out=par[:], in_=par_d[:, :]),
                         lambda e: e.dma_start(out=pos_t[:], in_=pos_d[:, :])], writes=[d_par, d_pos])
    load_x(0)
    op("pool", lambda e: e.memset(Vc[:], 1.0), writes=d_V)
    op("dve", lambda e: e.tensor_copy(out=identb[:], in_=par[:, P_ID:P_ID + 128]), reads=[d_par], writes=[d_id])
    op("dve", lambda e: e.tensor_copy(out=trib[:], in_=par[:, P_TRI:P_TRI + 128]), reads=[d_par], writes=[d_tri])
    op("dve", lambda e: e.memset(negM[:], -SM_SHIFT), writes=[d_negM])
    op("dve", lambda e: e.memset(epsT[:], EPS), writes=[d_epsT])
    op("dve", lambda e: e.tensor_copy(out=posf[:], in_=pos_t[:]), reads=[d_pos], writes=[d_posf])
    op("dve", lambda e: e.tensor_tensor(out=tA[:], in0=posf[:, :, None].broadcast_to([128, NTILE, 16]),
                                         in1=par[:, None, P_INVF:P_INVF + 16].broadcast_to([128, NTILE, 16]), op=ALU.mult),
       reads=[d_posf, d_par], writes=[d_tA])
    TWO_PI = 2.0 * math.pi
    C1 = 6.28125
    C2 = TWO_PI - C1
    for (shift, dst, d_dst) in ((0.0, sinT, d_sin), (0.5 * math.pi, cosT, d_cos)):
        op("dve", lambda e, shift=shift: e.tensor_scalar(out=tB[:], in0=tA[:], scalar1=shift, scalar2=None, op0=ALU.add),
           reads=[d_tA], writes=[d_tB])
        op("dve", lambda e: e.tensor_scalar(out=tI[:], in0=tB[:], scalar1=1.0 / TWO_PI, scalar2=None, op0=ALU.mult),
           reads=[d_tB], writes=[d_tI])
        op("dve", lambda e: e.tensor_copy(out=tC[:], in_=tI[:]), reads=[d_tI], writes=[d_tC])
        op("dve", lambda e: e.scalar_tensor_tensor(out=tB[:], in0=tC[:], scalar=-C1, in1=tB[:], op0=ALU.mult, op1=ALU.add),
           reads=[d_tC, d_tB], writes=[d_tB])
        op("dve", lambda e: e.scalar_tensor_tensor(out=tB[:], in0=tC[:], scalar=-C2, in1=tB[:], op0=ALU.mult, op1=ALU.add),
           reads=[d_tC, d_tB], writes=[d_tB])
        op("dve", lambda e: e.tensor_scalar(out=tB[:], in0=tB[:], scalar1=-3.1415925, scalar2=3.1415925, op0=ALU.max, op1=ALU.min),
           reads=[d_tB], writes=[d_tB])
        op("act", lambda e, dst=dst: e.activation(out=dst[:], in_=tB[:], func=AF.Sin), reads=[d_tB], writes=[d_dst])

    s_dbg = fw.new_dma_sem("dbg")

    def dump(i, xb):
        dst = dbg_d[i].rearrange("(t p) d -> p t d", p=128)
        fw.dma("sp", s_dbg, [lambda e, xb=xb, dst=dst: e.dma_start(out=dst, in_=xres[xb][:])], reads=d_x[xb])

    try:
        checkpoint("setup")
        for g in range(ng):
            xb = g % 2
            mx = mixer(xb, g, same_norm=P_G2)
            ffn(xb, "f1", P_G1, do_norm=(g == 0), same_norm=P_GM, tail_hook=lambda mx=mx: next(mx))
            checkpoint("ffn1")
            if dbg and g == 0:
                dump(0, xb)
            if g + 1 < ng:
                load_x(g + 1)
            for _ in mx:
                pass
            checkpoint("mixer")
            if dbg and g == 0:
                dump(1, xb)
            ffn(xb, "f2", P_G2, do_norm=False, hoist_norm=((1 - xb, P_G1) if g + 1 < ng else None))
            store_x(g)
    except StopBuild:
        store_x(0)
    fw.final_wait("sp", s_xout + [s_dbg])
    assert stop is not None or wstate["next_use"] == npieces, (wstate, npieces)
    fw.replay()
    return nc, WTOT, fw


_CACHE = {}


def kernel(**inputs):
    x = np.asarray(inputs["x"], np.float32)
    positions = np.asarray(inputs["positions"], np.int32)
    if "nc" not in _CACHE:
        _CACHE["nc"] = build_program()
    nc, WTOT, fw = _CACHE["nc"]
    ws = pack_weights(inputs)
    assert ws.shape == (128, WTOT), (ws.shape, WTOT)
    par = pack_params(inputs)
    in_maps = []
    for c in range(NCORES):
        xc = np.ascontiguousarray(x[2 * c:2 * c + 2].reshape(TOK, D))
        pc = np.ascontiguousarray(positions[2 * c:2 * c + 2].reshape(NTILE, 128).T)
        in_maps.append({"x": xc, "pos": pc, "par": par, "ws": ws})
    res = run_bass_kernel_spmd(nc, in_maps, core_ids=list(range(NCORES)))
    out = np.stack([np.asarray(r["out"], np.float32).reshape(2, S, D) for r in res.results], axis=0)
    return out.reshape(16, S, D)
```
